# Optimizing a Trainium2 kernel written in Bass

```python
import math
import jax, jax.numpy as jnp
from jax import lax
import numpy as np

D_MODEL = 1024
BATCH = 8
SEQ = 2048
DEPTH = 4
DEC_BATCH = 8
DEC_SEQ = 16
PAST_LEN = 4096

CHUNK = 64
N_META = 16
N_EVEN = (DEPTH + 1) // 2
N_ODD = DEPTH // 2
A_WIDTH = D_MODEL
A_CONV_W = 3
SB_HEADS = 16
SB_HEAD_DIM = 64
SB_WIDTH = SB_HEADS * SB_HEAD_DIM
Q_BLOCK = 128
EVEN_IN = 4 * A_WIDTH + 4 * SB_WIDTH
EVEN_SPLITS = (A_WIDTH, 2 * A_WIDTH, 3 * A_WIDTH, 4 * A_WIDTH,
               4 * A_WIDTH + SB_WIDTH, 4 * A_WIDTH + 2 * SB_WIDTH, 4 * A_WIDTH + 3 * SB_WIDTH)
EVEN_MIX = A_WIDTH + SB_WIDTH
C_WIDTH = 2 * D_MODEL
C_CONV_W = 31
ODD_IN = 3 * C_WIDTH
ALPHA = (2 * DEPTH) ** 0.25
INIT_BETA = (8 * DEPTH) ** -0.25
LN_EPS = 1e-5

kernel_name = "streaming_shortconv_stickbreak_conformer_trunk"


def layer_norm(x, g, b):
    xf = x.astype(jnp.float32)
    mu = jnp.mean(xf, axis=-1, keepdims=True)
    var = jnp.mean(jnp.square(xf - mu), axis=-1, keepdims=True)
    y = (xf - mu) * lax.rsqrt(var + LN_EPS) * g.astype(jnp.float32) + b.astype(jnp.float32)
    return y.astype(x.dtype)


def causal_dwconv(u, buf, w):
    width, ch = w.shape
    ext = jnp.concatenate([buf.astype(u.dtype), u], axis=1)
    y = lax.conv_general_dilated(ext, w[:, None, :].astype(u.dtype), window_strides=(1,),
                                 padding='VALID', dimension_numbers=('NWC', 'WIO', 'NWC'),
                                 feature_group_count=ch)
    return y, ext[:, ext.shape[1] - (width - 1):]


def stick_breaking(q, k, v, q_pos, k_pos):
    z = jnp.einsum('bhqd,bhkd->bhqk', q, k).astype(jnp.float32) * (1.0 / math.sqrt(SB_HEAD_DIM))
    mask = k_pos[None, :] < q_pos[:, None]
    log_beta = jax.nn.log_sigmoid(z)
    log_1m_beta = jnp.where(mask, log_beta - z, 0.0)
    between = lax.cumsum(log_1m_beta, axis=3, reverse=True) - log_1m_beta
    a = jnp.where(mask, jnp.exp(log_beta + between), 0.0)
    return jnp.einsum('bhqk,bhkd->bhqd', a.astype(v.dtype), v)


def stick_breaking_prompt(q, k, v):
    b, h, length, dh = q.shape
    n_blk = -(-length // Q_BLOCK)
    qp = jnp.pad(q, ((0, 0), (0, 0), (0, n_blk * Q_BLOCK - length), (0, 0)))
    q_blocks = qp.reshape(b, h, n_blk, Q_BLOCK, dh).transpose(2, 0, 1, 3, 4)
    k_pos = jnp.arange(length, dtype=jnp.int32)

    def one_block(args):
        qb, start = args
        return stick_breaking(qb, k, v, start + jnp.arange(Q_BLOCK, dtype=jnp.int32), k_pos)

    starts = jnp.arange(n_blk, dtype=jnp.int32) * Q_BLOCK
    out = lax.map(one_block, (q_blocks, starts))
    out = out.transpose(1, 2, 0, 3, 4).reshape(b, h, n_blk * Q_BLOCK, dh)
    return out[:, :, :length]


def split_heads(t):
    b, n, _ = t.shape
    return t.reshape(b, n, SB_HEADS, SB_HEAD_DIM).transpose(0, 2, 1, 3)


def even_mixer(x, conv_buf, k_past, v_past, w_in, conv_w, w_out):
    b, n, _ = x.shape
    h, gate_b, gate_c, z_a, q, k, v, z_b = jnp.split(x @ w_in, EVEN_SPLITS, axis=-1)
    conv_out, new_buf = causal_dwconv(gate_c * h, conv_buf, conv_w)
    y_a = gate_b * conv_out * jax.nn.silu(z_a)
    qh, kh, vh = split_heads(q), split_heads(k), split_heads(v)
    if k_past is None:
        o = stick_breaking_prompt(qh, kh, vh)
    else:
        p = k_past.shape[2]
        o = stick_breaking(qh, jnp.concatenate([k_past.astype(kh.dtype), kh], axis=2),
                           jnp.concatenate([v_past.astype(vh.dtype), vh], axis=2),
                           p + jnp.arange(n, dtype=jnp.int32), jnp.arange(p + n, dtype=jnp.int32))
    y_b = o.transpose(0, 2, 1, 3).reshape(b, n, SB_WIDTH) * jax.nn.silu(z_b)
    out = jnp.concatenate([y_a, y_b], axis=-1) @ w_out
    return out, new_buf, kh, vh


def odd_mixer(x, conv_buf, w_in, conv_w, conv_b, ln_g, ln_b, w_out):
    a, g, z_c = jnp.split(x @ w_in, 3, axis=-1)
    u = a * jax.nn.sigmoid(g)
    c, new_buf = causal_dwconv(u, conv_buf, conv_w)
    c = c + conv_b
    y = jax.nn.silu(layer_norm(c, ln_g, ln_b)) * jax.nn.silu(z_c)
    return y @ w_out, new_buf


def run_trunk(x, conv_a_bufs, conv_c_bufs, k_pasts, v_pasts, w_in_even, conv_a_w, w_out_even,
              w_in_odd, conv_c_w, conv_c_b, ln_c_g, ln_c_b, w_out_odd, post_ln_g, post_ln_b):
    new_k, new_v, new_a, new_c = [], [], [], []
    for layer in range(DEPTH):
        i = layer // 2
        if layer % 2 == 0:
            kp = None if k_pasts is None else k_pasts[i]
            vp = None if v_pasts is None else v_pasts[i]
            out, buf, kh, vh = even_mixer(x, conv_a_bufs[i], kp, vp, w_in_even[i], conv_a_w[i], w_out_even[i])
            new_k.append(kh)
            new_v.append(vh)
            new_a.append(buf)
        else:
            out, buf = odd_mixer(x, conv_c_bufs[i], w_in_odd[i], conv_c_w[i], conv_c_b[i],
                                 ln_c_g[i], ln_c_b[i], w_out_odd[i])
            new_c.append(buf)
        x = layer_norm(ALPHA * x + out, post_ln_g[layer], post_ln_b[layer])
    return x, jnp.stack(new_k), jnp.stack(new_v), jnp.stack(new_a), jnp.stack(new_c)


def setup_inputs(seed: int = 0) -> dict:
    key = jax.random.key(seed)
    ks = jax.random.split(key, 20)
    f32 = jnp.float32

    def nrm(k, shape, scale):
        return jax.random.normal(k, shape, f32) * scale

    return {
        "x_prompt": nrm(ks[0], (BATCH, SEQ, D_MODEL), 1.0),
        "x_sample": nrm(ks[1], (DEC_BATCH, DEC_SEQ, D_MODEL), 1.0),
        "cache_sb_k": nrm(ks[2], (N_EVEN, DEC_BATCH, SB_HEADS, N_META + PAST_LEN, SB_HEAD_DIM), 1.0),
        "cache_sb_v": nrm(ks[3], (N_EVEN, DEC_BATCH, SB_HEADS, N_META + PAST_LEN, SB_HEAD_DIM), 1.0),
        "state_conv_a": nrm(ks[4], (N_EVEN, DEC_BATCH, A_CONV_W - 1, A_WIDTH), 1.0),
        "state_conv_c": nrm(ks[5], (N_ODD, DEC_BATCH, C_CONV_W - 1, C_WIDTH), 1.0),
        "meta_tokens": nrm(ks[6], (N_META, D_MODEL), 1.0),
        "w_in_even": nrm(ks[7], (N_EVEN, D_MODEL, EVEN_IN), D_MODEL ** -0.5),
        "conv_a_w": nrm(ks[8], (N_EVEN, A_CONV_W, A_WIDTH), A_CONV_W ** -0.5),
        "w_out_even": nrm(ks[9], (N_EVEN, EVEN_MIX, D_MODEL), INIT_BETA * EVEN_MIX ** -0.5),
        "w_in_odd": nrm(ks[10], (N_ODD, D_MODEL, ODD_IN), D_MODEL ** -0.5),
        "conv_c_w": nrm(ks[11], (N_ODD, C_CONV_W, C_WIDTH), C_CONV_W ** -0.5),
        "conv_c_b": nrm(ks[12], (N_ODD, C_WIDTH), 0.02),
        "ln_c_g": 1.0 + nrm(ks[13], (N_ODD, C_WIDTH), 0.02),
        "ln_c_b": nrm(ks[14], (N_ODD, C_WIDTH), 0.02),
        "w_out_odd": nrm(ks[15], (N_ODD, C_WIDTH, D_MODEL), INIT_BETA * C_WIDTH ** -0.5),
        "post_ln_g": 1.0 + nrm(ks[16], (DEPTH, D_MODEL), 0.02),
        "post_ln_b": nrm(ks[17], (DEPTH, D_MODEL), 0.02),
    }


def reference(x_prompt, x_sample, cache_sb_k, cache_sb_v, state_conv_a, state_conv_c, meta_tokens,
              w_in_even, conv_a_w, w_out_even, w_in_odd, conv_c_w, conv_c_b, ln_c_g, ln_c_b,
              w_out_odd, post_ln_g, post_ln_b):
    weights = (w_in_even, conv_a_w, w_out_even, w_in_odd, conv_c_w, conv_c_b, ln_c_g, ln_c_b,
               w_out_odd, post_ln_g, post_ln_b)
    b_p = x_prompt.shape[0]
    meta = jnp.broadcast_to(meta_tokens.astype(x_prompt.dtype)[None], (b_p, N_META, D_MODEL))
    xp = jnp.concatenate([meta, x_prompt], axis=1)
    zeros_a = jnp.zeros((N_EVEN, b_p, A_CONV_W - 1, A_WIDTH), x_prompt.dtype)
    zeros_c = jnp.zeros((N_ODD, b_p, C_CONV_W - 1, C_WIDTH), x_prompt.dtype)
    h_p, k_p, v_p, a_p, c_p = run_trunk(xp, zeros_a, zeros_c, None, None, *weights)
    y_prompt = h_p[:, N_META:]
    y_sample, k_s, v_s, a_s, c_s = run_trunk(x_sample, state_conv_a, state_conv_c,
                                             cache_sb_k, cache_sb_v, *weights)
    return (y_prompt, y_sample, k_p, v_p, a_p, c_p, k_s, v_s, a_s, c_s)
```

```python
import numpy as np
from contextlib import ExitStack
import concourse.bass as bass
import concourse.mybir as mybir
from concourse.bass_utils import run_bass_kernel_spmd

F32 = mybir.dt.float32
BF16 = mybir.dt.bfloat16
AF = mybir.ActivationFunctionType
ALU = mybir.AluOpType

NDMASEM = 12


def _region(ap):
    tn = type(ap.tensor).__name__
    if tn.startswith("DRam"):
        return None
    dims = list(ap.ap)
    pstride, pcnt = dims[0]
    off = ap.offset
    if pstride == 0:
        return (ap.tensor.name, 0, 128, 0, 1 << 40)
    p0 = off // pstride
    f0 = off - p0 * pstride
    ext = 0
    for st, cnt in dims[1:]:
        ext += (cnt - 1) * abs(st)
    es = mybir.dt.size(ap.dtype)
    if tn.startswith("PSum"):
        return (ap.tensor.name, (p0 // 32) * 32, ((p0 + pcnt + 31) // 32) * 32, 0, 1 << 40)
    return (ap.tensor.name, p0, p0 + pcnt, f0 * es, (f0 + ext + 1) * es)


def _ovl(a, b):
    return a[1] < b[2] and b[1] < a[2] and a[3] < b[4] and b[3] < a[4]


def _cov(a, b):
    return a[1] <= b[1] and a[2] >= b[2] and a[3] <= b[3] and a[4] >= b[4]


class _Op:
    __slots__ = ("eng", "fn", "deps", "sig", "sigval", "is_dma", "dsem", "dval", "thr", "gi")


class Prog:
    def __init__(self, nc):
        self.nc = nc
        self.stack = ExitStack()
        self.ops = []
        self.wr = {}
        self.rd = {}
        self.ndma = {"sp": 0, "pool": 0, "act": 0}
        self.dma_sems = {}
        self.nbuf = 0

    def sb(self, name, shape, dtype):
        return self.stack.enter_context(self.nc.sbuf_tensor(name, list(shape), dtype))

    def ps(self, name, shape, dtype):
        return self.stack.enter_context(self.nc.psum_tensor(name, list(shape), dtype))

    def _rec(self, eng, fn, reads, writes, is_dma=False):
        op = _Op()
        op.eng = eng
        op.fn = fn
        op.is_dma = is_dma
        op.sig = False
        op.sigval = 0
        op.dsem = None
        op.dval = 0
        op.thr = None
        gi = len(self.ops)
        op.gi = gi
        deps = set()
        teng = "dma" if is_dma else eng
        writes = list(writes) + [ap for ap in reads if type(ap.tensor).__name__.startswith("PSum")]
        reads = [ap for ap in reads if not type(ap.tensor).__name__.startswith("PSum")]
        for ap in reads:
            r = _region(ap)
            if r is None:
                continue
            for w in self.wr.get(r[0], ()):
                if _ovl(w[0], r):
                    deps.add(w[1])
            lst = self.rd.setdefault(r[0], [])
            if teng != "dma":
                lst[:] = [x for x in lst if not (x[2] == teng and _cov(r, x[0]))]
            lst.append((r, gi, teng))
        for ap in writes:
            r = _region(ap)
            if r is None:
                continue
            wl = self.wr.setdefault(r[0], [])
            rl = self.rd.setdefault(r[0], [])
            for w in wl:
                if _ovl(w[0], r):
                    deps.add(w[1])
            for x in rl:
                if _ovl(x[0], r) and x[1] != gi:
                    deps.add(x[1])
            wl[:] = [w for w in wl if not _cov(r, w[0])]
            rl[:] = [x for x in rl if not _cov(r, x[0]) or x[1] == gi]
            wl.append((r, gi))
        deps.discard(gi)
        op.deps = deps
        self.ops.append(op)
        return op

    def mm(self, out, lhsT, rhs, start=True, stop=True, **kw):
        rd = [lhsT, rhs] if start else [lhsT, rhs, out]
        return self._rec("pe", lambda e: e.matmul(out, lhsT, rhs, start=start, stop=stop, **kw), rd, [out])

    def transpose(self, out, in_, ident):
        return self._rec("pe", lambda e: e.transpose(out, in_, ident), [in_, ident], [out])

    def act(self, out, in_, func, bias=0.0, scale=1.0, eng="act"):
        rd = [in_]
        if not isinstance(bias, (int, float)):
            rd.append(bias)
        if not isinstance(scale, (int, float)):
            rd.append(scale)
        return self._rec(eng, lambda e: e.activation(out, in_, func, bias=bias, scale=scale), rd, [out])

    def tt(self, eng, out, in0, in1, op):
        return self._rec(eng, lambda e: e.tensor_tensor(out, in0, in1, op), [in0, in1], [out])

    def ts(self, eng, out, in0, s1, s2, op0, op1=None):
        rd = [in0]
        if not isinstance(s1, (int, float)):
            rd.append(s1)
        if s2 is not None and not isinstance(s2, (int, float)):
            rd.append(s2)
        if op1 is None:
            return self._rec(eng, lambda e: e.tensor_scalar(out, in0, s1, None, op0), rd, [out])
        return self._rec(eng, lambda e: e.tensor_scalar(out, in0, s1, s2, op0, op1), rd, [out])

    def stt(self, eng, out, in0, scalar, in1, op0, op1):
        rd = [in0, in1]
        if not isinstance(scalar, (int, float)):
            rd.append(scalar)
        return self._rec(eng, lambda e: e.scalar_tensor_tensor(out, in0, scalar, in1, op0, op1), rd, [out])

    def copy(self, eng, out, in_):
        if eng == "act":
            return self._rec(eng, lambda e: e.copy(out, in_), [in_], [out])
        return self._rec(eng, lambda e: e.tensor_copy(out, in_), [in_], [out])

    def recip(self, eng, out, in_):
        return self._rec(eng, lambda e: e.reciprocal(out, in_), [in_], [out])

    def memset(self, eng, ap, val):
        return self._rec(eng, lambda e: e.memset(ap, val), [], [ap])

    def custom(self, eng, fn, reads, writes):
        return self._rec(eng, fn, reads, writes)

    def dma(self, q, out, in_):
        op = self._rec(q, lambda e: e.dma_start(out=out, in_=in_), [in_], [out], is_dma=True)
        i = self.ndma[q]
        self.ndma[q] = i + 1
        op.dsem = (q, i % NDMASEM)
        op.dval = 16 * (i // NDMASEM + 1)
        if i >= NDMASEM:
            op.thr = ((q, i % NDMASEM), 16 * (i // NDMASEM))
        return op

    def finish(self):
        nc = self.nc
        ops = self.ops
        engs = ["pe", "act", "dve", "pool", "sp"]
        for op in ops:
            best = {}
            keep = set()
            for d in op.deps:
                dop = ops[d]
                if dop.is_dma:
                    keep.add(d)
                    continue
                if dop.eng == "pe" and op.eng == "pe" and not op.is_dma:
                    continue
                if d > best.get(dop.eng, -1):
                    best[dop.eng] = d
            for d in best.values():
                ops[d].sig = True
                keep.add(d)
            op.deps = keep
        cnt = {e: 0 for e in engs}
        for op in ops:
            if op.sig and not op.is_dma:
                cnt[op.eng] += 1
                op.sigval = cnt[op.eng]
        esem = {e: self.stack.enter_context(nc.semaphore("s_" + e)) for e in engs}
        dsem = {}
        for q in ("sp", "pool", "act"):
            for k in range(min(NDMASEM, self.ndma[q])):
                dsem[(q, k)] = self.stack.enter_context(nc.semaphore("d_%s_%d" % (q, k)))
        per = {e: [] for e in engs}
        for op in ops:
            per[op.eng].append(op)
        dfinal = {}
        for op in ops:
            if op.is_dma:
                dfinal[op.dsem] = max(dfinal.get(op.dsem, 0), op.dval)

        def emit(ename, e):
            waited = {}

            def wait(key, sem, val):
                if waited.get(key, 0) >= val:
                    return
                waited[key] = val
                e.wait_ge(sem, val)

            for op in per[ename]:
                for d in sorted(op.deps):
                    dop = ops[d]
                    if dop.is_dma:
                        wait(dop.dsem, dsem[dop.dsem], dop.dval)
                    else:
                        if dop.eng == "pe" and ename == "pe" and not op.is_dma:
                            continue
                        wait(dop.eng, esem[dop.eng], dop.sigval)
                if op.is_dma:
                    if op.thr is not None:
                        wait(op.thr[0], dsem[op.thr[0]], op.thr[1])
                    op.fn(e).then_inc(dsem[op.dsem], 16)
                else:
                    ins = op.fn(e)
                    if op.sig:
                        ins.then_inc(esem[ename], 1)
            for key, val in dfinal.items():
                if key[0] == ename:
                    wait(key, dsem[key], val)

        with nc.allow_non_contiguous_dma(reason="small strided state/param transfers"), nc.Block() as block:
            @block.tensor
            def _(e):
                emit("pe", e)

            @block.scalar
            def _(e):
                emit("act", e)

            @block.vector
            def _(e):
                emit("dve", e)

            @block.gpsimd
            def _(e):
                emit("pool", e)

            @block.sync
            def _(e):
                emit("sp", e)
        self.stack.close()
        return cnt


NCORES = 8
D = 1024
T = 2080
PO = 32
TILES = [(0, 32), (32, 512), (544, 512), (1056, 512), (1568, 512)]
ALPHA = float(8 ** 0.25)
EPS = 1e-5
PAST = 4112


CFG = {"layers": 4, "sub": 99, "pre": 15, "sblk": 33, "sdbg": 99}


def build_program():
    nc = bass.Bass("TRN2", target_bir_lowering=False)

    def din(name, shape):
        return nc.dram_tensor(name, list(shape), F32, kind="ExternalInput")

    def dout(name, shape):
        return nc.dram_tensor(name, list(shape), F32, kind="ExternalOutput")

    xin = din("xin", [T, D])
    ck = din("ck", [2, 16, PAST, 64])
    cv = din("cv", [2, 16, PAST, 64])
    sca = din("sca", [2, 2, 1024])
    scc = din("scc", [2, 30, 2048])
    w_in_even = din("w_in_even", [2, 1024, 8192])
    conv_a_w = din("conv_a_w", [2, 3, 1024])
    w_out_even = din("w_out_even", [2, 2048, 1024])
    w_in_odd = din("w_in_odd", [2, 1024, 6144])
    conv_c_w = din("conv_c_w", [2, 31, 2048])
    conv_c_b = din("conv_c_b", [2, 2048])
    ln_c_g = din("ln_c_g", [2, 2048])
    ln_c_b = din("ln_c_b", [2, 2048])
    w_out_odd = din("w_out_odd", [2, 2048, 1024])
    post_ln_g = din("post_ln_g", [4, 1024])
    post_ln_b = din("post_ln_b", [4, 1024])
    y_o = dout("y", [T, D])
    kp_o = dout("kp", [2, 16, 2064, 64])
    vp_o = dout("vp", [2, 16, 2064, 64])
    ks_o = dout("ks", [2, 16, 16, 64])
    vs_o = dout("vs", [2, 16, 16, 64])
    cap_o = dout("cap", [2, 2, 1024])
    ccp_o = dout("ccp", [2, 30, 2048])
    cas_o = dout("cas", [2, 2, 1024])
    ccs_o = dout("ccs", [2, 30, 2048])

    P = Prog(nc)
    X = P.sb("X", [128, 8, T], F32)
    XB = P.sb("XB", [128, 8, T], BF16)
    WIN = [P.sb("WIN%d" % k, [128, 8, 256], BF16) for k in range(2)]
    WOUT = P.sb("WOUT", [128, 4, 1024], BF16)
    AR = P.sb("AR", [128, 9792], F32)
    ARb = AR.bitcast(BF16)
    TT = P.sb("TT", [128, 8, 512], F32)
    LB = [P.sb("LB%d" % k, [128, 512], BF16) for k in range(6)]
    SZT = P.sb("SZT", [128, 2, 512], F32)
    SZ0T = P.sb("SZ0T", [128, 32], F32)
    ST0 = P.sb("ST0", [128, 2, 32], F32)
    AB = [P.sb("AB%d" % k, [128, 512], BF16) for k in range(2)]
    UB = P.sb("UB", [128, 2090], BF16)
    DG = P.sb("DG", [128, 31, 128], BF16)
    IDF = P.sb("IDF", [128, 128], F32)
    IDB = P.sb("IDB", [128, 128], BF16)
    NTm = P.sb("NTm", [128, 128], BF16)
    NCm = P.sb("NCm", [128, 128], BF16)
    M0 = P.sb("M0", [128, 512], F32)
    ONES = P.sb("ONES", [128, 128], BF16)
    QS = P.sb("QS", [128, 8, 16], BF16)
    KS = P.sb("KS", [128, 4, 16], BF16)
    VS = P.sb("VS", [16, 4, 128], BF16)
    SZS = P.sb("SZS", [128, 4, 16], F32)
    USTA = P.sb("USTA", [128, 8, 4], F32)
    USTS = P.sb("USTS", [128, 16, 16], F32)
    USTP = P.sb("USTP", [128, 16, 30], F32)
    HIST = P.sb("HIST", [128, 16, 30], BF16)
    LNG = P.sb("LNG", [128, 8, 4], F32)
    LNB = P.sb("LNB", [128, 8, 4], F32)
    LCG = P.sb("LCG", [128, 16, 2], F32)
    LCB = P.sb("LCB", [128, 16, 2], F32)
    CVB = P.sb("CVB", [128, 16, 2], F32)
    WA = P.sb("WA", [128, 2, 8, 3], F32)
    WC = P.sb("WC", [128, 2, 16, 31], BF16)
    SA = P.sb("SA", [128, 2, 8, 2], F32)
    SC = P.sb("SC", [128, 2, 16, 30], BF16)
    PB = [P.ps("pb%d" % k, [128, 512], F32) for k in range(8)]

    QT = ARb[:, 0:2080]
    KT = ARb[:, 2080:4160]
    VB = ARb[:, 4160:6464].rearrange("p (b f) -> p b f", b=18)
    KTC = ARb[:, 6464:6976].rearrange("p (c s) -> p c s", c=4)
    VC = [ARb[:, 6976:7488], ARb[:, 7488:8000], ARb[:, 18368:18880], ARb[:, 18880:19392]]
    KC = [AR[:, 4000:4512], AR[:, 4512:5024]]
    Y4 = ARb[:, 10048:18368].rearrange("p (c t) -> p c t", c=4)
    CC = AR[:, 0:8704].rearrange("p (c t) -> p c t", c=16)
    YO = ARb[:, 17408:19584].rearrange("p (c t) -> p c t", c=4)

    st = {"rot": 0, "reserved": set()}

    def nb():
        while True:
            k = st["rot"] % 8
            st["rot"] += 1
            if k not in st["reserved"]:
                return PB[k]

    P.memset("pool", IDF[:], 1.0)
    P.custom("pool", lambda e: e.affine_select(out=IDF[:], in_=IDF[:], compare_op=ALU.is_ge, fill=0.0, base=0,
                                               pattern=[[-1, 128]], channel_multiplier=1), [IDF[:]], [IDF[:]])
    P.custom("pool", lambda e: e.affine_select(out=IDF[:], in_=IDF[:], compare_op=ALU.is_ge, fill=0.0, base=0,
                                               pattern=[[1, 128]], channel_multiplier=-1), [IDF[:]], [IDF[:]])
    P.copy("pool", IDB[:], IDF[:])
    P.memset("pool", TT[:, 0, 0:128], -1.0)
    P.custom("pool", lambda e: e.affine_select(out=TT[:, 0, 0:128], in_=TT[:, 0, 0:128], compare_op=ALU.is_ge, fill=0.0,
                                               base=0, pattern=[[-1, 128]], channel_multiplier=1),
             [TT[:, 0, 0:128]], [TT[:, 0, 0:128]])
    P.copy("pool", NTm[:], TT[:, 0, 0:128])
    P.memset("pool", TT[:, 1, 0:128], -1.0)
    P.custom("pool", lambda e: e.affine_select(out=TT[:, 1, 0:128], in_=TT[:, 1, 0:128], compare_op=ALU.is_gt, fill=0.0,
                                               base=0, pattern=[[1, 128]], channel_multiplier=-1),
             [TT[:, 1, 0:128]], [TT[:, 1, 0:128]])
    P.copy("pool", NCm[:], TT[:, 1, 0:128])
    P.memset("pool", M0[:], 1.0)
    P.custom("pool", lambda e: e.affine_select(out=M0[:], in_=M0[:], compare_op=ALU.is_gt, fill=0.0, base=0,
                                               pattern=[[1, 512]], channel_multiplier=-1), [M0[:]], [M0[:]])
    P.memset("pool", ONES[:], 1.0)
    P.memset("pool", QS[:], 0.0)

    def load_T(src, R, W, dest):
        nch = W // 128
        stg = TT[0:R, 4:8, :] if W == 2048 else TT[0:R, 4:6, :]
        P.dma("sp", stg, src.rearrange("r (a b) -> r a b", b=512))
        pb = nb()
        for c in range(nch):
            P.transpose(pb[:, c * R:(c + 1) * R], TT[0:R, 4 + c // 4, (c % 4) * 128:(c % 4 + 1) * 128], IDF[0:R, 0:R])
        P.copy("dve", dest, pb[:, 0:nch * R].rearrange("p (c r) -> p c r", r=R))

    def store_T(src_fn, R, W, dst):
        nch = W // 128
        for g in range(nch // 4):
            pb = nb()
            for cc in range(4):
                P.transpose(pb[0:R, cc * 128:(cc + 1) * 128], src_fn(4 * g + cc), IDF[:, :])
            P.copy("dve", TT[0:R, 4 + g, :], pb[0:R, :])
        stg = TT[0:R, 4:4 + nch // 4, :]
        P.dma("sp", dst.rearrange("r (a b) -> r a b", b=512), stg)

    if CFG["pre"] & 2:
      load_T(post_ln_g[:, :], 4, 1024, LNG[:, :, :])
      load_T(post_ln_b[:, :], 4, 1024, LNB[:, :, :])
      load_T(ln_c_g[:, :], 2, 2048, LCG[:, :, :])
      load_T(ln_c_b[:, :], 2, 2048, LCB[:, :, :])
      load_T(conv_c_b[:, :], 2, 2048, CVB[:, :, :])
      for i in range(2):
        load_T(conv_a_w[i, :, :], 3, 1024, WA[:, i, :, :])
        load_T(conv_c_w[i, :, :], 31, 2048, WC[:, i, :, :])
        load_T(sca[i, :, :], 2, 1024, SA[:, i, :, :])
        load_T(scc[i, :, :], 30, 2048, SC[:, i, :, :])

    blocks_in = [(0, 32)] + [(PO + 128 * j, 128) for j in range(16)]
    for bi, (r0, nr) in enumerate(blocks_in if CFG["pre"] & 4 else []):
        stg = TT[0:nr, (bi % 2) * 2:(bi % 2) * 2 + 2, :]
        P.dma("sp", stg, xin[r0:r0 + nr, :].rearrange("r (a b) -> r a b", b=512))
        for g in range(2):
            pb = nb()
            for cc in range(4):
                P.transpose(pb[:, cc * 128:cc * 128 + nr], TT[0:nr, (bi % 2) * 2 + g, cc * 128:(cc + 1) * 128],
                            IDF[0:nr, 0:nr])
            src = pb[:, :].rearrange("p (c t) -> p c t", c=4)[:, :, 0:nr]
            P.copy("dve", X[:, 4 * g:4 * g + 4, r0:r0 + nr], src)
            P.act(XB[:, 4 * g:4 * g + 4, r0:r0 + nr], src, AF.Copy)

    def ld_w(buf, half, wsrc, col0):
        P.dma("pool", buf[:, :, half * 128:(half + 1) * 128],
              wsrc[:, col0:col0 + 128].rearrange("(k p) n -> p k n", p=128))

    def proj_fm(ps, wbuf, wc0, col0, n):
        for k in range(8):
            P.mm(ps[:, 0:n], wbuf[:, k, wc0:wc0 + 128], XB[:, k, col0:col0 + n], start=(k == 0), stop=(k == 7))

    def bcast_diag(ntaps, wap_tensor, woff):
        in0 = bass.AP(IDB, 0, [[128, 128], [0, ntaps], [1, 128]])
        in1 = bass.AP(wap_tensor, woff, [[wap_tensor.shape[1] * (wap_tensor.shape[2] if len(wap_tensor.shape) > 2 else 1)
                                          * (wap_tensor.shape[3] if len(wap_tensor.shape) > 3 else 1), 128],
                                         [1, ntaps], [0, 128]])
        if ntaps > 8:
            h = ntaps // 2
            ps_ = in1.ap[0][0]
            in0a = bass.AP(IDB, 0, [[128, 128], [0, h], [1, 128]])
            in0b = bass.AP(IDB, 0, [[128, 128], [0, ntaps - h], [1, 128]])
            in1a = bass.AP(wap_tensor, woff, [[ps_, 128], [1, h], [0, 128]])
            in1b = bass.AP(wap_tensor, woff + h, [[ps_, 128], [1, ntaps - h], [0, 128]])
            P.tt("dve", DG[:, 0:h, :], in0a, in1a, ALU.mult)
            P.tt("dve", DG[:, h:ntaps, :], in0b, in1b, ALU.mult)
        else:
            P.tt("dve", DG[:, 0:ntaps, :], in0, in1, ALU.mult)

    def post_ln(tiles, l):
        for ti, (c0, n) in enumerate(tiles):
            ps_s = nb()
            ps_q = nb()
            for oc in range(8):
                rb = LB[oc % 2][:, 0:n]
                rq = AB[oc % 2][:, 0:n]
                P.act(rb, X[:, oc, c0:c0 + n], AF.Copy)
                P.act(rq, X[:, oc, c0:c0 + n], AF.Square)
                P.mm(ps_s[:, 0:n], ONES[:, :], rb, start=(oc == 0), stop=(oc == 7))
                P.mm(ps_q[:, 0:n], ONES[:, :], rq, start=(oc == 0), stop=(oc == 7))
            mu = TT[:, 0, 0:n]
            rs = TT[:, 1, 0:n]
            m2 = TT[:, 2, 0:n]
            P.act(mu, ps_s[:, 0:n], AF.Copy, scale=1.0 / 1024)
            P.act(rs, ps_q[:, 0:n], AF.Copy, scale=1.0 / 1024)
            P.tt("dve", m2, mu, mu, ALU.mult)
            P.tt("dve", rs, rs, m2, ALU.subtract)
            P.act(rs, rs, AF.Ln, bias=EPS)
            P.act(rs, rs, AF.Exp, scale=-0.5)
            for oc in range(8):
                t = TT[:, 3 + oc % 2, 0:n]
                xs = X[:, oc, c0:c0 + n]
                P.tt("dve", t, xs, mu, ALU.subtract)
                P.tt("dve", t, t, rs, ALU.mult)
                P.act(xs, t, AF.Identity, bias=LNB[:, oc, l:l + 1], scale=LNG[:, oc, l:l + 1])
                P.copy("dve", XB[:, oc, c0:c0 + n], xs)

    def out_proj_acc(wsrc_rows, Ybuf, ncol_tiles, first):
        P.dma("pool", WOUT[:, :, :], wsrc_rows.rearrange("(k p) n -> p k n", p=128))
        for oc in range(8):
            for (c0, n, y0) in ncol_tiles:
                pb = nb()
                for k in range(4):
                    P.mm(pb[:, 0:n], WOUT[:, k, oc * 128:(oc + 1) * 128], Ybuf[:, k, y0:y0 + n],
                         start=(k == 0), stop=(k == 3))
                xs = X[:, oc, c0:c0 + n]
                if first:
                    P.stt("dve", xs, xs, ALPHA, pb[:, 0:n], ALU.mult, ALU.add)
                else:
                    P.tt("dve", xs, xs, pb[:, 0:n], ALU.add)

    def run_streams(gens):
        gens = list(gens)
        while gens:
            for g in list(gens):
                try:
                    next(g)
                except StopIteration:
                    gens.remove(g)

    def att_stream(sid, segs):
        ps_s, ps_r, ps_o = PB[3 * sid], PB[3 * sid + 1], PB[3 * sid + 2]
        Es = [TT[:, 4 * sid + k, :] for k in range(3)]
        ER = TT[:, 4 * sid + 3, :]
        Lbs = [LB[3 * sid + k] for k in range(3)]
        Ab = AB[sid]
        items = []
        for sg in segs:
            nblk = len(sg["blocks"])
            for j, blk in enumerate(sg["blocks"]):
                items.append((sg, blk, j, j == 0, j == nblk - 1))
        n = len(items)

        def do_qk(k):
            sg, (nk, c0, mfn, bi), j, first, last = items[k]
            if first and sg.get("pre") is not None:
                sg["pre"]()
            sg["qk"](ps_s, bi, nk, c0, j)
        do_qk(0)
        yield
        for i in range(n + 1):
            if i < n:
                sg, (nk, c0, mfn, bi), j, first, last = items[i]
                NQ = sg["NQ"]
                E = Es[i % 3]
                P.act(E[0:nk, c0:NQ], ps_s[0:nk, c0:NQ], AF.Exp)
                if mfn is not None:
                    mfn(E, nk, c0)
                P.act(Lbs[i % 3][0:nk, c0:NQ], E[0:nk, c0:NQ], AF.Ln, bias=1.0)
            yield
            if i >= 1:
                sg, (nk, c0, mfn, bi), j, first, last = items[i - 1]
                NQ = sg["NQ"]
                Lb = Lbs[(i - 1) % 3]
                P.mm(ps_r[:, c0:NQ], NTm[0:nk, :], Lb[0:nk, c0:NQ], start=first, stop=False, skip_group_check=True)
            yield
            if i + 1 < n:
                do_qk(i + 1)
            yield
            if i >= 1:
                P.act(ER[0:nk, c0:NQ], ps_r[0:nk, c0:NQ], AF.Exp)
            yield
            if i >= 1 and not last:
                P.mm(ps_r[:, c0:NQ], NCm[0:nk, :], Lb[0:nk, c0:NQ], start=False, stop=False, skip_group_check=True)
            yield
            if i >= 1:
                E = Es[(i - 1) % 3]
                P.tt("dve", Ab[0:nk, c0:NQ], ER[0:nk, c0:NQ], E[0:nk, c0:NQ], ALU.mult)
                sg["av"](ps_o, Ab, bi, nk, c0, j, first)
                if last:
                    sg["fin"](ps_o)
            yield

    def even_layer(l):
        i = l // 2
        wsrc = w_in_even[i, :, :]
        for g in range(2):
            ld_w(WIN[0], 0, wsrc, (4 * g) * 128)
            ld_w(WIN[0], 1, wsrc, 2048 + (4 * g) * 128)
            for cc in range(4):
                c = 4 * g + cc
                ld_w(WIN[1], 0, wsrc, 1024 + c * 128)
                ld_w(WIN[1], 1, wsrc, 3072 + c * 128)
                bcast_diag(3, WA, ((i * 8 + c) * 3))
                P.copy("dve", UB[:, 0:2], SA[:, i, c, :])
                P.memset("dve", UB[:, 18:20], 0.0)
                for ti, (c0, n) in enumerate(TILES):
                    ps_h = nb()
                    ps_gc = nb()
                    proj_fm(ps_h, WIN[0], 0, c0, n)
                    proj_fm(ps_gc, WIN[0], 128, c0, n)
                    hs = TT[:, ti % 2, 0:n]
                    P.act(hs, ps_h[:, 0:n], AF.Copy)
                    if ti == 0:
                        P.tt("dve", UB[:, 2:18], ps_gc[:, 0:16], hs[:, 0:16], ALU.mult)
                        P.tt("dve", UB[:, 20:36], ps_gc[:, 16:32], hs[:, 16:32], ALU.mult)
                        P.tt("dve", USTA[:, c, 0:2], ps_gc[:, 14:16], hs[:, 14:16], ALU.mult)
                    else:
                        P.tt("dve", UB[:, c0 + 4:c0 + 4 + n], ps_gc[:, 0:n], hs, ALU.mult)
                    if ti == 4:
                        P.tt("dve", USTA[:, c, 2:4], ps_gc[:, 510:512], hs[:, 510:512], ALU.mult)
                    ps_gb = nb()
                    ps_za = nb()
                    proj_fm(ps_gb, WIN[1], 0, c0, n)
                    proj_fm(ps_za, WIN[1], 128, c0, n)
                    ps_cv = nb()
                    if ti == 0:
                        for j in range(3):
                            P.mm(ps_cv[:, 0:16], DG[:, j, :], UB[:, j:j + 16], start=(j == 0), stop=(j == 2))
                        for j in range(3):
                            P.mm(ps_cv[:, 16:32], DG[:, j, :], UB[:, 18 + j:18 + j + 16], start=(j == 0), stop=(j == 2),
                                 skip_group_check=True)
                    else:
                        for j in range(3):
                            P.mm(ps_cv[:, 0:n], DG[:, j, :], UB[:, c0 + 2 + j:c0 + 2 + j + n], start=(j == 0), stop=(j == 2))
                    sz = TT[:, 2 + 2 * (ti % 2), 0:n]
                    g2 = TT[:, 3 + 2 * (ti % 2), 0:n]
                    P.act(sz, ps_za[:, 0:n], AF.Silu)
                    P.tt("dve", g2, ps_gb[:, 0:n], sz, ALU.mult)
                    P.tt("dve", Y4[:, cc, c0:c0 + n], ps_cv[:, 0:n], g2, ALU.mult)
                if cc < 3:
                    ld_w(WIN[0], 0, wsrc, (c + 1) * 128)
                    ld_w(WIN[0], 1, wsrc, 2048 + (c + 1) * 128)
            out_proj_acc(w_out_even[i, (4 * g) * 128:(4 * g + 4) * 128, :], Y4,
                         [(c0, n, c0) for (c0, n) in TILES], first=(g == 0))
        if CFG["sub"] < 1:
            return
        store_T(lambda c: USTA[:, c, 2:4], 2, 1024, cap_o[i, :, :])
        store_T(lambda c: USTA[:, c, 0:2], 2, 1024, cas_o[i, :, :])

        tblocks = [(0, 16), (16, 16)] + [(PO + 128 * j, 128) for j in range(16)]
        if CFG["sub"] < 2:
            return
        for g in range(2):
            for cc in range(4):
                c = 4 * g + cc
                ld_w(WIN[0], 0, wsrc, 4096 + c * 128)
                ld_w(WIN[0], 1, wsrc, 5120 + c * 128)
                ld_w(WIN[1], 0, wsrc, 6144 + c * 128)
                ld_w(WIN[1], 1, wsrc, 7168 + c * 128)
                for (c0, n) in TILES:
                    ps_q = nb()
                    ps_k = nb()
                    proj_fm(ps_q, WIN[0], 0, c0, n)
                    proj_fm(ps_k, WIN[0], 128, c0, n)
                    P.act(QT[:, c0:c0 + n], ps_q[:, 0:n], AF.Copy, scale=0.125)
                    P.copy("dve", KT[:, c0:c0 + n], ps_k[:, 0:n])
                P.copy("dve", QS[0:64, 2 * cc, :], QT[0:64, 0:16])
                P.copy("dve", QS[64:128, 2 * cc + 1, :], QT[64:128, 0:16])
                P.copy("dve", KS[:, cc, :], KT[:, 0:16])
                for which in range(2):
                    wbuf, wc0 = (WIN[0], 128) if which == 0 else (WIN[1], 0)
                    o_p = kp_o if which == 0 else vp_o
                    o_s = ks_o if which == 0 else vs_o
                    for grp in range(5):
                        tbs = [0, 1] if grp == 0 else [2 + 4 * (grp - 1) + k for k in range(4)]
                        pb = nb()
                        for bi2, tb in enumerate(tbs):
                            t0, nt = tblocks[tb]
                            for k in range(8):
                                P.mm(pb[0:nt, bi2 * 128:(bi2 + 1) * 128], XB[:, k, t0:t0 + nt],
                                     wbuf[:, k, wc0:wc0 + 128], start=(k == 0), stop=(k == 7), skip_group_check=True)
                        stg = TT[:, 6 + (grp % 2), :]
                        if grp == 0:
                            P.copy("dve", stg[0:16, 0:256], pb[0:16, 0:256])
                            for hh in range(2):
                                h = 2 * c + hh
                                P.dma("sp", o_s[i, h, :, :], stg[0:16, hh * 64:(hh + 1) * 64])
                                P.dma("sp", o_p[i, h, 0:16, :], stg[0:16, 128 + hh * 64:128 + (hh + 1) * 64])
                            if which == 1:
                                P.copy("dve", VB[0:16, 0, :], stg[0:16, 0:128])
                                P.copy("dve", VB[0:16, 1, :], stg[0:16, 128:256])
                                P.copy("dve", VS[0:16, cc, :], stg[0:16, 0:128])
                        else:
                            P.copy("dve", stg[:, :], pb[:, :])
                            q0 = 16 + 512 * (grp - 1)
                            for hh in range(2):
                                h = 2 * c + hh
                                P.dma("sp", o_p[i, h, q0:q0 + 512, :].rearrange("(j p) d -> p j d", p=128),
                                      stg.rearrange("p (j f) -> p j f", j=4)[:, :, hh * 64:(hh + 1) * 64])
                            if which == 1:
                                tb0 = 2 + 4 * (grp - 1)
                                P.copy("dve", VB[:, tb0:tb0 + 4, :], stg.rearrange("p (j f) -> p j f", j=4))
                SZ0 = SZ0T[:, :]

                def pre0(cc=cc):
                    pz = PB[6]
                    proj_fm(pz, WIN[1], 128, 0, 32)
                    P.act(SZ0, pz[:, 0:32], AF.Exp, scale=-1.0)
                    P.ts("dve", SZ0, SZ0, 1.0, None, ALU.add)
                    P.recip("dve", SZ0, SZ0)
                    P.tt("dve", SZ0, SZ0, pz[:, 0:32], ALU.mult)
                    P.copy("dve", SZS[:, cc, :], SZ0[:, 0:16])

                def mk_stream(hh, cc=cc):
                    rows = slice(hh * 64, (hh + 1) * 64)
                    segs = []

                    def qk_m(ps_s, bi, nk, c0, j):
                        P.mm(ps_s[0:16, 0:16], KT[rows, 16:32], QT[rows, 16:32], start=True, stop=True)

                    def av_m(ps_o, Ab, bi, nk, c0, j, first):
                        P.mm(ps_o[:, 0:16], VB[0:16, 1, :], Ab[0:16, 0:16], start=first, stop=False, skip_group_check=True)

                    def mfn_m(E, nk, c0):
                        P.tt("dve", E[0:16, 0:16], E[0:16, 0:16], M0[0:16, 0:16], ALU.mult)

                    def fin_m(ps_o):
                        P.tt("dve", Y4[rows, cc, 16:32], ps_o[rows, 0:16], SZ0[rows, 16:32], ALU.mult)
                    segs.append(dict(NQ=16, blocks=[(16, 0, mfn_m, "M")], qk=qk_m, av=av_m, fin=fin_m,
                                     pre=(pre0 if hh == 0 else None)))
                    for qi in range(4):
                        qc0 = PO + 512 * qi
                        SZ = SZT[:, qi % 2, :]

                        def pre(qc0=qc0, SZ=SZ):
                            pz = PB[6]
                            proj_fm(pz, WIN[1], 128, qc0, 512)
                            P.act(SZ, pz[:, :], AF.Exp, scale=-1.0)
                            P.ts("dve", SZ, SZ, 1.0, None, ALU.add)
                            P.recip("dve", SZ, SZ)
                            P.tt("dve", SZ, SZ, pz[:, :], ALU.mult)

                        def mfn(E, nk, c0):
                            P.tt("dve", E[:, c0:512], E[:, c0:512], M0[:, 0:512 - c0], ALU.mult)
                        blocks = []
                        for k in (3, 2, 1, 0):
                            blocks.append((128, 128 * k, mfn, 4 * qi + k))
                        for j in range(4 * qi - 1, -1, -1):
                            blocks.append((128, 0, None, j))
                        blocks.append((16, 0, None, "M"))

                        def qk(ps_s, bi, nk, c0, j, qc0=qc0):
                            kc0 = 16 if bi == "M" else PO + 128 * bi
                            P.mm(ps_s[0:nk, c0:512], KT[rows, kc0:kc0 + nk], QT[rows, qc0 + c0:qc0 + 512],
                                 start=True, stop=True)

                        def av(ps_o, Ab, bi, nk, c0, j, first):
                            tb = 1 if bi == "M" else 2 + bi
                            P.mm(ps_o[:, c0:512], VB[0:nk, tb, :], Ab[0:nk, c0:512], start=first, stop=False,
                                 skip_group_check=True)

                        def fin(ps_o, qc0=qc0, SZ=SZ):
                            P.tt("dve", Y4[rows, cc, qc0:qc0 + 512], ps_o[rows, :], SZ[rows, :], ALU.mult)
                        segs.append(dict(NQ=512, blocks=blocks, qk=qk, av=av, fin=fin, pre=(pre if hh == 0 else None)))
                    return att_stream(hh, segs)
                run_streams([mk_stream(0), mk_stream(1)])

            def sample_stream():
                NQ = 128
                blocks = [(16, 0, "S", "S")] + [(16 if b == 32 else 128, 0, None, b) for b in range(32, -1, -1)][:CFG["sblk"]]

                def qk(ps_s, bi, nk, c0, j):
                    if bi == "S":
                        for hq in range(8):
                            cq, hh = hq // 2, hq % 2
                            P.mm(ps_s[0:16, hq * 16:(hq + 1) * 16], KS[:, cq, :], QS[:, hq, :], start=True, stop=True,
                                 skip_group_check=True)
                        return
                    r0 = 128 * bi

                    def kdma(jj):
                        nkk, _, _, bb = blocks[jj]
                        P.dma("sp", KC[jj % 2][0:nkk, :].rearrange("s (h d) -> s h d", h=8),
                              ck[i, 8 * g:8 * g + 8, 128 * bb:128 * bb + nkk, :].rearrange("h s d -> s h d"))
                    if j == 1:
                        kdma(1)
                    P.dma("pool", VC[j % 4][0:nk, :].rearrange("s (h d) -> s h d", h=8),
                          cv[i, 8 * g:8 * g + 8, r0:r0 + nk, :].rearrange("h s d -> s h d"))
                    pt = PB[7]
                    for cq in range(4):
                        P.transpose(pt[:, cq * 128:cq * 128 + nk], KC[j % 2][0:nk, cq * 128:(cq + 1) * 128], IDF[0:nk, 0:nk])
                    P.copy("dve", KTC[:, :, 0:nk], pt[:, :].rearrange("p (c s) -> p c s", c=4)[:, :, 0:nk])
                    if j + 1 < len(blocks):
                        kdma(j + 1)
                    for hq in range(8):
                        cq, hh = hq // 2, hq % 2
                        P.mm(ps_s[0:nk, hq * 16:(hq + 1) * 16], KTC[:, cq, 0:nk], QS[:, hq, :], start=True, stop=True,
                             skip_group_check=True)

                def av(ps_o, Ab, bi, nk, c0, j, first):
                    for hq in range(8):
                        cq = hq // 2
                        if bi == "S":
                            lhs = VS[0:16, cq, :]
                        else:
                            lhs = VC[j % 4][0:nk, cq * 128:(cq + 1) * 128]
                        P.mm(ps_o[:, hq * 16:(hq + 1) * 16], lhs, Ab[0:nk, hq * 16:(hq + 1) * 16],
                             start=(first and hq == 0), stop=False, skip_group_check=True)

                def mfn(E, nk, c0):
                    m = bass.AP(M0, 0, [[512, 16], [0, 8], [1, 16]])
                    ev = E[0:16, 0:128].rearrange("p (h t) -> p h t", h=8)
                    P.tt("dve", ev, ev, m, ALU.mult)

                def fin(ps_o):
                    for hq in range(8):
                        cq, hh = hq // 2, hq % 2
                        rows = slice(hh * 64, (hh + 1) * 64)
                        P.tt("dve", Y4[rows, cq, 0:16], ps_o[rows, hq * 16:(hq + 1) * 16], SZS[rows, cq, :], ALU.mult)
                blocks = [(nk, c0, (mfn if m else None), bi) for (nk, c0, m, bi) in blocks]
                return att_stream(0, [dict(NQ=NQ, blocks=blocks, qk=qk, av=av, fin=fin, pre=None)])
            if CFG["sub"] < 5:
                continue
            run_streams([sample_stream()])
            if CFG["sub"] < 6:
                continue
            out_proj_acc(w_out_even[i, 1024 + (4 * g) * 128:1024 + (4 * g + 4) * 128, :], Y4,
                         [(c0, n, c0) for (c0, n) in TILES], first=False)
        if CFG["sub"] < 7:
            return
        post_ln(TILES, l)

    def odd_layer(l):
        i = l // 2
        wsrc = w_in_odd[i, :, :]
        groups = [[TILES[0], TILES[1]], [TILES[2]], [TILES[3]], [TILES[4]]]
        for gi, gt in enumerate(groups):
            gc0 = gt[0][0]
            ncols = sum(n for _, n in gt)
            nst = len(gt)
            st["reserved"] = set()
            ps_sum = [PB[0], PB[1]]
            ps_sq = [PB[2], PB[1]]
            st["reserved"] = {0, 1, 2}
            ld_w(WIN[0], 0, wsrc, 0)
            ld_w(WIN[0], 1, wsrc, 2048)
            ld_w(WIN[1], 0, wsrc, 128)
            ld_w(WIN[1], 1, wsrc, 2048 + 128)
            items = [(c, ti) for c in range(16) for ti in range(nst)]
            pend = {}

            def proj_item(k):
                c, ti = items[k]
                c0, n = gt[ti]
                ps_a = nb()
                ps_g = nb()
                proj_fm(ps_a, WIN[c % 2], 0, c0, n)
                proj_fm(ps_g, WIN[c % 2], 128, c0, n)
                pend[k] = (ps_a, ps_g)
                if ti == nst - 1 and c + 2 < 16:
                    ld_w(WIN[c % 2], 0, wsrc, (c + 2) * 128)
                    ld_w(WIN[c % 2], 1, wsrc, 2048 + (c + 2) * 128)
            proj_item(0)
            for k, (c, ti) in enumerate(items):
                if k + 1 < len(items):
                    proj_item(k + 1)
                c0, n = gt[ti]
                ps_a, ps_g = pend.pop(k)
                if ti == 0:
                    bcast_diag(31, WC, ((i * 16 + c) * 31))
                    if gi == 0:
                        P.copy("dve", UB[:, 0:30], SC[:, i, c, :])
                        P.memset("dve", UB[:, 64:94], 0.0)
                    else:
                        P.copy("dve", UB[:, 64:94], HIST[:, c, :])
                sg = TT[:, k % 4, 0:n]
                P.act(sg, ps_g[:, 0:n], AF.Sigmoid)
                lc0 = c0 - gc0
                if gi == 0 and ti == 0:
                    P.tt("dve", UB[:, 30:46], ps_a[:, 0:16], sg[:, 0:16], ALU.mult)
                    P.tt("dve", UB[:, 94:110], ps_a[:, 16:32], sg[:, 16:32], ALU.mult)
                    P.tt("dve", USTS[:, c, :], ps_a[:, 0:16], sg[:, 0:16], ALU.mult)
                    ub0 = None
                else:
                    ub0 = 94 + (lc0 - 16 if gi == 0 else lc0)
                    P.tt("dve", UB[:, ub0:ub0 + n], ps_a[:, 0:n], sg, ALU.mult)
                if gi == 3:
                    P.tt("dve", USTP[:, c, :], ps_a[:, 482:512], sg[:, 482:512], ALU.mult)
                ps_cv = nb()
                if ub0 is None:
                    for j in range(31):
                        P.mm(ps_cv[:, 0:16], DG[:, j, :], UB[:, j:j + 16], start=(j == 0), stop=(j == 30))
                    for j in range(31):
                        P.mm(ps_cv[:, 16:32], DG[:, j, :], UB[:, 64 + j:64 + j + 16], start=(j == 0), stop=(j == 30),
                             skip_group_check=True)
                else:
                    for j in range(31):
                        P.mm(ps_cv[:, 0:n], DG[:, j, :], UB[:, ub0 - 30 + j:ub0 - 30 + j + n],
                             start=(j == 0), stop=(j == 30))
                cs = CC[:, c, lc0:lc0 + n]
                P.act(cs, ps_cv[:, 0:n], AF.Identity, bias=CVB[:, c, i:i + 1])
                cb = LB[k % 2][:, 0:n]
                cq = AB[k % 2][:, 0:n]
                P.act(cb, cs, AF.Copy)
                P.act(cq, cs, AF.Square)
                if nst == 2 and ti == 0:
                    P.mm(PB[1][:, 0:32], ONES[:, :], cb, start=(c == 0), stop=(c == 15), skip_group_check=True)
                    P.mm(PB[1][:, 32:64], ONES[:, :], cq, start=False, stop=(c == 15), skip_group_check=True)
                else:
                    P.mm(PB[0][:, 0:n], ONES[:, :], cb, start=(c == 0), stop=(c == 15), skip_group_check=True)
                    P.mm(PB[2][:, 0:n], ONES[:, :], cq, start=(c == 0), stop=(c == 15), skip_group_check=True)
                if gi < 3 and ti == nst - 1:
                    lastcol = 94 + (ncols - 16 if gi == 0 else ncols)
                    P.copy("dve", HIST[:, c, :], UB[:, lastcol - 30:lastcol])
            stats = []
            for ti, (c0, n) in enumerate(gt):
                if n == 32:
                    mu, rs = ST0[:, 0, :], ST0[:, 1, :]
                else:
                    mu, rs = SZT[:, 0, 0:n], SZT[:, 1, 0:n]
                if nst == 2 and ti == 0:
                    s_ap, q_ap = PB[1][:, 0:32], PB[1][:, 32:64]
                else:
                    s_ap, q_ap = PB[0][:, 0:n], PB[2][:, 0:n]
                P.act(mu, s_ap, AF.Copy, scale=1.0 / 2048)
                P.act(rs, q_ap, AF.Copy, scale=1.0 / 2048)
                m2 = TT[:, 0, 0:n]
                P.tt("dve", m2, mu, mu, ALU.mult)
                P.tt("dve", rs, rs, m2, ALU.subtract)
                P.act(rs, rs, AF.Ln, bias=EPS)
                P.act(rs, rs, AF.Exp, scale=-0.5)
                stats.append((mu, rs))
            st["reserved"] = set()
            ld_w(WIN[0], 0, wsrc, 4096)
            ld_w(WIN[0], 1, wsrc, 4096 + 128)

            def zproj(c):
                wb = WIN[(c // 2) % 2]
                if c % 2 == 0 and c < 14:
                    ld_w(WIN[(c // 2 + 1) % 2], 0, wsrc, 4096 + (c + 2) * 128)
                    ld_w(WIN[(c // 2 + 1) % 2], 1, wsrc, 4096 + (c + 3) * 128)
                res = []
                for ti, (c0, n) in enumerate(gt):
                    pz = PB[(c % 2) * 2 + ti]
                    proj_fm(pz, wb, (c % 2) * 128, c0, n)
                    res.append(pz)
                return res
            st["reserved"] = {0, 1, 2, 3}
            pzs_next = zproj(0)
            for c in range(16):
                pzs = pzs_next
                if c + 1 < 16:
                    pzs_next = zproj(c + 1)
                for ti, (c0, n) in enumerate(gt):
                    lc0 = c0 - gc0
                    mu, rs = stats[ti]
                    pz = pzs[ti]
                    sz = TT[:, 3 * (c % 2), 0:n]
                    P.act(sz, pz[:, 0:n], AF.Silu)
                    t = TT[:, 3 * (c % 2) + 1, 0:n]
                    cs = CC[:, c, lc0:lc0 + n]
                    P.tt("dve", t, cs, mu, ALU.subtract)
                    P.tt("dve", t, t, rs, ALU.mult)
                    s2 = TT[:, 3 * (c % 2) + 2, 0:n]
                    P.act(s2, t, AF.Silu, bias=LCB[:, c, i:i + 1], scale=LCG[:, c, i:i + 1])
                    P.tt("dve", YO[:, c % 4, lc0:lc0 + n], s2, sz, ALU.mult)
                if c % 4 == 3:
                    k0 = (c // 4) * 4
                    out_proj_acc(w_out_odd[i, k0 * 128:(k0 + 4) * 128, :], YO,
                                 [(c0, n, c0 - gc0) for (c0, n) in gt], first=(c == 3))
            st["reserved"] = set()
            post_ln(gt, l)
        store_T(lambda c: USTP[:, c, :], 30, 2048, ccp_o[i, :, :])
        P.dma("sp", ccs_o[i, 0:14, :], scc[i, 16:30, :])
        store_T(lambda c: USTS[:, c, :], 16, 2048, ccs_o[i, 14:30, :])

    for l in range(CFG["layers"]):
        if l % 2 == 0:
            even_layer(l)
        else:
            odd_layer(l)

    for bi, (r0, nr) in enumerate(blocks_in if CFG["pre"] & 8 else []):
        for g in range(2):
            pb = nb()
            for cc in range(4):
                P.transpose(pb[0:nr, cc * 128:(cc + 1) * 128], X[:, 4 * g + cc, r0:r0 + nr], IDF[:, :])
            P.copy("dve", TT[0:nr, (bi % 2) * 2 + g, :], pb[0:nr, :])
        P.dma("sp", y_o[r0:r0 + nr, :].rearrange("r (a b) -> r a b", b=512), TT[0:nr, (bi % 2) * 2:(bi % 2) * 2 + 2, :])

    P.finish()
    return nc


_NC_CACHE = {}


def kernel(x_prompt, x_sample, cache_sb_k, cache_sb_v, state_conv_a, state_conv_c, meta_tokens,
           w_in_even, conv_a_w, w_out_even, w_in_odd, conv_c_w, conv_c_b, ln_c_g, ln_c_b,
           w_out_odd, post_ln_g, post_ln_b):
    f = lambda a: np.ascontiguousarray(np.asarray(a, dtype=np.float32))
    x_prompt, x_sample, cache_sb_k, cache_sb_v = f(x_prompt), f(x_sample), f(cache_sb_k), f(cache_sb_v)
    state_conv_a, state_conv_c, meta_tokens = f(state_conv_a), f(state_conv_c), f(meta_tokens)
    shared = {"w_in_even": f(w_in_even), "conv_a_w": f(conv_a_w), "w_out_even": f(w_out_even),
              "w_in_odd": f(w_in_odd), "conv_c_w": f(conv_c_w), "conv_c_b": f(conv_c_b), "ln_c_g": f(ln_c_g),
              "ln_c_b": f(ln_c_b), "w_out_odd": f(w_out_odd), "post_ln_g": f(post_ln_g), "post_ln_b": f(post_ln_b)}
    if "nc" not in _NC_CACHE:
        _NC_CACHE["nc"] = build_program()
    nc = _NC_CACHE["nc"]
    in_maps = []
    for b in range(NCORES):
        m = dict(shared)
        m["xin"] = np.ascontiguousarray(np.concatenate([x_sample[b], meta_tokens, x_prompt[b]], axis=0))
        m["ck"] = np.ascontiguousarray(cache_sb_k[:, b])
        m["cv"] = np.ascontiguousarray(cache_sb_v[:, b])
        m["sca"] = np.ascontiguousarray(state_conv_a[:, b])
        m["scc"] = np.ascontiguousarray(state_conv_c[:, b])
        in_maps.append(m)
    res = run_bass_kernel_spmd(nc, in_maps, core_ids=list(range(NCORES)))
    R = res.results
    y = np.stack([r["y"] for r in R], 0)
    y_prompt = np.ascontiguousarray(y[:, PO:, :])
    y_sample = np.ascontiguousarray(y[:, 0:16, :])
    st = lambda k: np.ascontiguousarray(np.stack([r[k] for r in R], 1)).astype(np.float32)
    return (y_prompt, y_sample, st("kp"), st("vp"), st("cap"), st("ccp"), st("ks"), st("vs"), st("cas"), st("ccs"))
```

```python
import numpy as np
from contextlib import ExitStack
import concourse.bass as bass
import concourse.mybir as mybir
from concourse.bass_utils import run_bass_kernel_spmd

F32 = mybir.dt.float32
BF16 = mybir.dt.bfloat16
AF = mybir.ActivationFunctionType
ALU = mybir.AluOpType

NDMASEM = 12


def _region(ap):
    tn = type(ap.tensor).__name__
    if tn.startswith("DRam"):
        return None
    dims = list(ap.ap)
    pstride, pcnt = dims[0]
    off = ap.offset
    if pstride == 0:
        return (ap.tensor.name, 0, 128, 0, 1 << 40)
    p0 = off // pstride
    f0 = off - p0 * pstride
    ext = 0
    for st, cnt in dims[1:]:
        ext += (cnt - 1) * abs(st)
    es = mybir.dt.size(ap.dtype)
    if tn.startswith("PSum"):
        return (ap.tensor.name, (p0 // 32) * 32, ((p0 + pcnt + 31) // 32) * 32, 0, 1 << 40)
    return (ap.tensor.name, p0, p0 + pcnt, f0 * es, (f0 + ext + 1) * es)


def _ovl(a, b):
    return a[1] < b[2] and b[1] < a[2] and a[3] < b[4] and b[3] < a[4]


def _cov(a, b):
    return a[1] <= b[1] and a[2] >= b[2] and a[3] <= b[3] and a[4] >= b[4]


class _Op:
    __slots__ = ("eng", "fn", "deps", "sig", "sigval", "is_dma", "dsem", "dval", "thr", "gi")


class Prog:
    def __init__(self, nc):
        self.nc = nc
        self.stack = ExitStack()
        self.ops = []
        self.wr = {}
        self.rd = {}
        self.ndma = {"sp": 0, "pool": 0, "act": 0}
        self.dma_sems = {}
        self.nbuf = 0

    def sb(self, name, shape, dtype):
        return self.stack.enter_context(self.nc.sbuf_tensor(name, list(shape), dtype))

    def ps(self, name, shape, dtype):
        return self.stack.enter_context(self.nc.psum_tensor(name, list(shape), dtype))

    def _rec(self, eng, fn, reads, writes, is_dma=False):
        op = _Op()
        op.eng = eng
        op.fn = fn
        op.is_dma = is_dma
        op.sig = False
        op.sigval = 0
        op.dsem = None
        op.dval = 0
        op.thr = None
        gi = len(self.ops)
        op.gi = gi
        deps = set()
        teng = "dma" if is_dma else eng
        writes = list(writes) + [ap for ap in reads if type(ap.tensor).__name__.startswith("PSum")]
        reads = [ap for ap in reads if not type(ap.tensor).__name__.startswith("PSum")]
        for ap in reads:
            r = _region(ap)
            if r is None:
                continue
            for w in self.wr.get(r[0], ()):
                if _ovl(w[0], r):
                    deps.add(w[1])
            lst = self.rd.setdefault(r[0], [])
            if teng != "dma":
                lst[:] = [x for x in lst if not (x[2] == teng and _cov(r, x[0]))]
            lst.append((r, gi, teng))
        for ap in writes:
            r = _region(ap)
            if r is None:
                continue
            wl = self.wr.setdefault(r[0], [])
            rl = self.rd.setdefault(r[0], [])
            for w in wl:
                if _ovl(w[0], r):
                    deps.add(w[1])
            for x in rl:
                if _ovl(x[0], r) and x[1] != gi:
                    deps.add(x[1])
            wl[:] = [w for w in wl if not _cov(r, w[0])]
            rl[:] = [x for x in rl if not _cov(r, x[0]) or x[1] == gi]
            wl.append((r, gi))
        deps.discard(gi)
        op.deps = deps
        self.ops.append(op)
        return op

    def mm(self, out, lhsT, rhs, start=True, stop=True, **kw):
        rd = [lhsT, rhs] if start else [lhsT, rhs, out]
        return self._rec("pe", lambda e: e.matmul(out, lhsT, rhs, start=start, stop=stop, **kw), rd, [out])

    def transpose(self, out, in_, ident):
        return self._rec("pe", lambda e: e.transpose(out, in_, ident), [in_, ident], [out])

    def act(self, out, in_, func, bias=0.0, scale=1.0, eng="act"):
        rd = [in_]
        if not isinstance(bias, (int, float)):
            rd.append(bias)
        if not isinstance(scale, (int, float)):
            rd.append(scale)
        return self._rec(eng, lambda e: e.activation(out, in_, func, bias=bias, scale=scale), rd, [out])

    def tt(self, eng, out, in0, in1, op):
        return self._rec(eng, lambda e: e.tensor_tensor(out, in0, in1, op), [in0, in1], [out])

    def ts(self, eng, out, in0, s1, s2, op0, op1=None):
        rd = [in0]
        if not isinstance(s1, (int, float)):
            rd.append(s1)
        if s2 is not None and not isinstance(s2, (int, float)):
            rd.append(s2)
        if op1 is None:
            return self._rec(eng, lambda e: e.tensor_scalar(out, in0, s1, None, op0), rd, [out])
        return self._rec(eng, lambda e: e.tensor_scalar(out, in0, s1, s2, op0, op1), rd, [out])

    def stt(self, eng, out, in0, scalar, in1, op0, op1):
        rd = [in0, in1]
        if not isinstance(scalar, (int, float)):
            rd.append(scalar)
        return self._rec(eng, lambda e: e.scalar_tensor_tensor(out, in0, scalar, in1, op0, op1), rd, [out])

    def copy(self, eng, out, in_):
        if eng == "act":
            return self._rec(eng, lambda e: e.copy(out, in_), [in_], [out])
        return self._rec(eng, lambda e: e.tensor_copy(out, in_), [in_], [out])

    def recip(self, eng, out, in_):
        return self._rec(eng, lambda e: e.reciprocal(out, in_), [in_], [out])

    def memset(self, eng, ap, val):
        return self._rec(eng, lambda e: e.memset(ap, val), [], [ap])

    def custom(self, eng, fn, reads, writes):
        return self._rec(eng, fn, reads, writes)

    def dma(self, q, out, in_):
        op = self._rec(q, lambda e: e.dma_start(out=out, in_=in_), [in_], [out], is_dma=True)
        i = self.ndma[q]
        self.ndma[q] = i + 1
        op.dsem = (q, i % NDMASEM)
        op.dval = 16 * (i // NDMASEM + 1)
        if i >= NDMASEM:
            op.thr = ((q, i % NDMASEM), 16 * (i // NDMASEM))
        return op

    def finish(self):
        nc = self.nc
        ops = self.ops
        engs = ["pe", "act", "dve", "pool", "sp"]
        for op in ops:
            best = {}
            keep = set()
            for d in op.deps:
                dop = ops[d]
                if dop.is_dma:
                    keep.add(d)
                    continue
                if dop.eng == "pe" and op.eng == "pe" and not op.is_dma:
                    continue
                if d > best.get(dop.eng, -1):
                    best[dop.eng] = d
            for d in best.values():
                ops[d].sig = True
                keep.add(d)
            op.deps = keep
        cnt = {e: 0 for e in engs}
        for op in ops:
            if op.sig and not op.is_dma:
                cnt[op.eng] += 1
                op.sigval = cnt[op.eng]
        esem = {e: self.stack.enter_context(nc.semaphore("s_" + e)) for e in engs}
        dsem = {}
        for q in ("sp", "pool", "act"):
            for k in range(min(NDMASEM, self.ndma[q])):
                dsem[(q, k)] = self.stack.enter_context(nc.semaphore("d_%s_%d" % (q, k)))
        per = {e: [] for e in engs}
        for op in ops:
            per[op.eng].append(op)
        dfinal = {}
        for op in ops:
            if op.is_dma:
                dfinal[op.dsem] = max(dfinal.get(op.dsem, 0), op.dval)

        def emit(ename, e):
            waited = {}

            def wait(key, sem, val):
                if waited.get(key, 0) >= val:
                    return
                waited[key] = val
                e.wait_ge(sem, val)

            for op in per[ename]:
                for d in sorted(op.deps):
                    dop = ops[d]
                    if dop.is_dma:
                        wait(dop.dsem, dsem[dop.dsem], dop.dval)
                    else:
                        if dop.eng == "pe" and ename == "pe" and not op.is_dma:
                            continue
                        wait(dop.eng, esem[dop.eng], dop.sigval)
                if op.is_dma:
                    if op.thr is not None:
                        wait(op.thr[0], dsem[op.thr[0]], op.thr[1])
                    op.fn(e).then_inc(dsem[op.dsem], 16)
                else:
                    ins = op.fn(e)
                    if op.sig:
                        ins.then_inc(esem[ename], 1)
            for key, val in dfinal.items():
                if key[0] == ename:
                    wait(key, dsem[key], val)

        with nc.allow_non_contiguous_dma(reason="small strided state/param transfers"), nc.Block() as block:
            @block.tensor
            def _(e):
                emit("pe", e)

            @block.scalar
            def _(e):
                emit("act", e)

            @block.vector
            def _(e):
                emit("dve", e)

            @block.gpsimd
            def _(e):
                emit("pool", e)

            @block.sync
            def _(e):
                emit("sp", e)
        self.stack.close()
        return cnt


NCORES = 8
D = 1024
T = 2080
PO = 32
TILES = [(0, 32), (32, 512), (544, 512), (1056, 512), (1568, 512)]
ALPHA = float(8 ** 0.25)
EPS = 1e-5
PAST = 4112


CFG = {"layers": 4, "sub": 99, "pre": 15, "sblk": 33, "sdbg": 99}


def build_program():
    nc = bass.Bass("TRN2", target_bir_lowering=False)

    def din(name, shape):
        return nc.dram_tensor(name, list(shape), F32, kind="ExternalInput")

    def dout(name, shape):
        return nc.dram_tensor(name, list(shape), F32, kind="ExternalOutput")

    xin = din("xin", [T, D])
    ck = din("ck", [2, 16, PAST, 64])
    cv = din("cv", [2, 16, PAST, 64])
    sca = din("sca", [2, 2, 1024])
    scc = din("scc", [2, 30, 2048])
    w_in_even = din("w_in_even", [2, 1024, 8192])
    conv_a_w = din("conv_a_w", [2, 3, 1024])
    w_out_even = din("w_out_even", [2, 2048, 1024])
    w_in_odd = din("w_in_odd", [2, 1024, 6144])
    conv_c_w = din("conv_c_w", [2, 31, 2048])
    conv_c_b = din("conv_c_b", [2, 2048])
    ln_c_g = din("ln_c_g", [2, 2048])
    ln_c_b = din("ln_c_b", [2, 2048])
    w_out_odd = din("w_out_odd", [2, 2048, 1024])
    post_ln_g = din("post_ln_g", [4, 1024])
    post_ln_b = din("post_ln_b", [4, 1024])
    y_o = dout("y", [T, D])
    kp_o = dout("kp", [2, 16, 2064, 64])
    vp_o = dout("vp", [2, 16, 2064, 64])
    ks_o = dout("ks", [2, 16, 16, 64])
    vs_o = dout("vs", [2, 16, 16, 64])
    cap_o = dout("cap", [2, 2, 1024])
    ccp_o = dout("ccp", [2, 30, 2048])
    cas_o = dout("cas", [2, 2, 1024])
    ccs_o = dout("ccs", [2, 30, 2048])

    P = Prog(nc)
    X = P.sb("X", [128, 8, T], F32)
    XB = P.sb("XB", [128, 8, T], BF16)
    WIN = [P.sb("WIN%d" % k, [128, 8, 256], BF16) for k in range(2)]
    WOUT = P.sb("WOUT", [128, 4, 1024], BF16)
    AR = P.sb("AR", [128, 9792], F32)
    ARb = AR.bitcast(BF16)
    TT = P.sb("TT", [128, 8, 512], F32)
    LB = [P.sb("LB%d" % k, [128, 512], BF16) for k in range(6)]
    SZT = P.sb("SZT", [128, 2, 512], F32)
    SZ0T = P.sb("SZ0T", [128, 32], F32)
    ST0 = P.sb("ST0", [128, 2, 32], F32)
    AB = [P.sb("AB%d" % k, [128, 512], BF16) for k in range(2)]
    UB = P.sb("UB", [128, 2090], BF16)
    DG = P.sb("DG", [128, 31, 128], BF16)
    IDF = P.sb("IDF", [128, 128], F32)
    IDB = P.sb("IDB", [128, 128], BF16)
    NTm = P.sb("NTm", [128, 128], BF16)
    NCm = P.sb("NCm", [128, 128], BF16)
    M0 = P.sb("M0", [128, 512], F32)
    ONES = P.sb("ONES", [128, 128], BF16)
    QS = P.sb("QS", [128, 8, 16], BF16)
    KS = P.sb("KS", [128, 4, 16], BF16)
    VS = P.sb("VS", [16, 4, 128], BF16)
    SZS = P.sb("SZS", [128, 4, 16], F32)
    USTA = P.sb("USTA", [128, 8, 4], F32)
    USTS = P.sb("USTS", [128, 16, 16], F32)
    USTP = P.sb("USTP", [128, 16, 30], F32)
    HIST = P.sb("HIST", [128, 16, 30], BF16)
    LNG = P.sb("LNG", [128, 8, 4], F32)
    LNB = P.sb("LNB", [128, 8, 4], F32)
    LCG = P.sb("LCG", [128, 16, 2], F32)
    LCB = P.sb("LCB", [128, 16, 2], F32)
    CVB = P.sb("CVB", [128, 16, 2], F32)
    WA = P.sb("WA", [128, 2, 8, 3], F32)
    WC = P.sb("WC", [128, 2, 16, 31], BF16)
    SA = P.sb("SA", [128, 2, 8, 2], F32)
    SC = P.sb("SC", [128, 2, 16, 30], BF16)
    PB = [P.ps("pb%d" % k, [128, 512], F32) for k in range(8)]

    QT = ARb[:, 0:2080]
    KT = ARb[:, 2080:4160]
    VB = ARb[:, 4160:6464].rearrange("p (b f) -> p b f", b=18)
    KTC = ARb[:, 6464:6976].rearrange("p (c s) -> p c s", c=4)
    VC = [ARb[:, 6976:7488], ARb[:, 7488:8000], ARb[:, 18368:18880], ARb[:, 18880:19392]]
    KC = [AR[:, 4000:4512], AR[:, 4512:5024]]
    Y4 = ARb[:, 10048:18368].rearrange("p (c t) -> p c t", c=4)
    CC = AR[:, 0:8704].rearrange("p (c t) -> p c t", c=16)
    YO = ARb[:, 17408:19584].rearrange("p (c t) -> p c t", c=4)

    st = {"rot": 0, "reserved": set()}

    def nb():
        while True:
            k = st["rot"] % 8
            st["rot"] += 1
            if k not in st["reserved"]:
                return PB[k]

    P.memset("pool", IDF[:], 1.0)
    P.custom("pool", lambda e: e.affine_select(out=IDF[:], in_=IDF[:], compare_op=ALU.is_ge, fill=0.0, base=0,
                                               pattern=[[-1, 128]], channel_multiplier=1), [IDF[:]], [IDF[:]])
    P.custom("pool", lambda e: e.affine_select(out=IDF[:], in_=IDF[:], compare_op=ALU.is_ge, fill=0.0, base=0,
                                               pattern=[[1, 128]], channel_multiplier=-1), [IDF[:]], [IDF[:]])
    P.copy("pool", IDB[:], IDF[:])
    P.memset("pool", TT[:, 0, 0:128], -1.0)
    P.custom("pool", lambda e: e.affine_select(out=TT[:, 0, 0:128], in_=TT[:, 0, 0:128], compare_op=ALU.is_ge, fill=0.0,
                                               base=0, pattern=[[-1, 128]], channel_multiplier=1),
             [TT[:, 0, 0:128]], [TT[:, 0, 0:128]])
    P.copy("pool", NTm[:], TT[:, 0, 0:128])
    P.memset("pool", TT[:, 1, 0:128], -1.0)
    P.custom("pool", lambda e: e.affine_select(out=TT[:, 1, 0:128], in_=TT[:, 1, 0:128], compare_op=ALU.is_gt, fill=0.0,
                                               base=0, pattern=[[1, 128]], channel_multiplier=-1),
             [TT[:, 1, 0:128]], [TT[:, 1, 0:128]])
    P.copy("pool", NCm[:], TT[:, 1, 0:128])
    P.memset("pool", M0[:], 1.0)
    P.custom("pool", lambda e: e.affine_select(out=M0[:], in_=M0[:], compare_op=ALU.is_gt, fill=0.0, base=0,
                                               pattern=[[1, 512]], channel_multiplier=-1), [M0[:]], [M0[:]])
    P.memset("pool", ONES[:], 1.0)
    P.memset("pool", QS[:], 0.0)

    def load_T(src, R, W, dest):
        nch = W // 128
        stg = TT[0:R, 4:8, :] if W == 2048 else TT[0:R, 4:6, :]
        P.dma("sp", stg, src.rearrange("r (a b) -> r a b", b=512))
        pb = nb()
        for c in range(nch):
            P.transpose(pb[:, c * R:(c + 1) * R], TT[0:R, 4 + c // 4, (c % 4) * 128:(c % 4 + 1) * 128], IDF[0:R, 0:R])
        P.copy("dve", dest, pb[:, 0:nch * R].rearrange("p (c r) -> p c r", r=R))

    def store_T(src_fn, R, W, dst):
        nch = W // 128
        for g in range(nch // 4):
            pb = nb()
            for cc in range(4):
                P.transpose(pb[0:R, cc * 128:(cc + 1) * 128], src_fn(4 * g + cc), IDF[:, :])
            P.copy("dve", TT[0:R, 4 + g, :], pb[0:R, :])
        stg = TT[0:R, 4:4 + nch // 4, :]
        P.dma("sp", dst.rearrange("r (a b) -> r a b", b=512), stg)

    if CFG["pre"] & 2:
      load_T(post_ln_g[:, :], 4, 1024, LNG[:, :, :])
      load_T(post_ln_b[:, :], 4, 1024, LNB[:, :, :])
      load_T(ln_c_g[:, :], 2, 2048, LCG[:, :, :])
      load_T(ln_c_b[:, :], 2, 2048, LCB[:, :, :])
      load_T(conv_c_b[:, :], 2, 2048, CVB[:, :, :])
      for i in range(2):
        load_T(conv_a_w[i, :, :], 3, 1024, WA[:, i, :, :])
        load_T(conv_c_w[i, :, :], 31, 2048, WC[:, i, :, :])
        load_T(sca[i, :, :], 2, 1024, SA[:, i, :, :])
        load_T(scc[i, :, :], 30, 2048, SC[:, i, :, :])

    blocks_in = [(0, 32)] + [(PO + 128 * j, 128) for j in range(16)]
    for bi, (r0, nr) in enumerate(blocks_in if CFG["pre"] & 4 else []):
        stg = TT[0:nr, (bi % 2) * 2:(bi % 2) * 2 + 2, :]
        P.dma("sp", stg, xin[r0:r0 + nr, :].rearrange("r (a b) -> r a b", b=512))
        for g in range(2):
            pb = nb()
            for cc in range(4):
                P.transpose(pb[:, cc * 128:cc * 128 + nr], TT[0:nr, (bi % 2) * 2 + g, cc * 128:(cc + 1) * 128],
                            IDF[0:nr, 0:nr])
            src = pb[:, :].rearrange("p (c t) -> p c t", c=4)[:, :, 0:nr]
            P.copy("dve", X[:, 4 * g:4 * g + 4, r0:r0 + nr], src)
            P.act(XB[:, 4 * g:4 * g + 4, r0:r0 + nr], src, AF.Copy)

    def ld_w(buf, half, wsrc, col0):
        P.dma("pool", buf[:, :, half * 128:(half + 1) * 128],
              wsrc[:, col0:col0 + 128].rearrange("(k p) n -> p k n", p=128))

    def proj_fm(ps, wbuf, wc0, col0, n):
        for k in range(8):
            P.mm(ps[:, 0:n], wbuf[:, k, wc0:wc0 + 128], XB[:, k, col0:col0 + n], start=(k == 0), stop=(k == 7))

    def bcast_diag(ntaps, wap_tensor, woff):
        in0 = bass.AP(IDB, 0, [[128, 128], [0, ntaps], [1, 128]])
        in1 = bass.AP(wap_tensor, woff, [[wap_tensor.shape[1] * (wap_tensor.shape[2] if len(wap_tensor.shape) > 2 else 1)
                                          * (wap_tensor.shape[3] if len(wap_tensor.shape) > 3 else 1), 128],
                                         [1, ntaps], [0, 128]])
        if ntaps > 8:
            h = ntaps // 2
            ps_ = in1.ap[0][0]
            in0a = bass.AP(IDB, 0, [[128, 128], [0, h], [1, 128]])
            in0b = bass.AP(IDB, 0, [[128, 128], [0, ntaps - h], [1, 128]])
            in1a = bass.AP(wap_tensor, woff, [[ps_, 128], [1, h], [0, 128]])
            in1b = bass.AP(wap_tensor, woff + h, [[ps_, 128], [1, ntaps - h], [0, 128]])
            P.tt("dve", DG[:, 0:h, :], in0a, in1a, ALU.mult)
            P.tt("dve", DG[:, h:ntaps, :], in0b, in1b, ALU.mult)
        else:
            P.tt("dve", DG[:, 0:ntaps, :], in0, in1, ALU.mult)

    def post_ln(tiles, l):
        for ti, (c0, n) in enumerate(tiles):
            ps_s = nb()
            ps_q = nb()
            for oc in range(8):
                rb = LB[oc % 2][:, 0:n]
                rq = AB[oc % 2][:, 0:n]
                P.act(rb, X[:, oc, c0:c0 + n], AF.Copy)
                P.act(rq, X[:, oc, c0:c0 + n], AF.Square)
                P.mm(ps_s[:, 0:n], ONES[:, :], rb, start=(oc == 0), stop=(oc == 7))
                P.mm(ps_q[:, 0:n], ONES[:, :], rq, start=(oc == 0), stop=(oc == 7))
            mu = TT[:, 0, 0:n]
            rs = TT[:, 1, 0:n]
            m2 = TT[:, 2, 0:n]
            P.act(mu, ps_s[:, 0:n], AF.Copy, scale=1.0 / 1024)
            P.act(rs, ps_q[:, 0:n], AF.Copy, scale=1.0 / 1024)
            P.tt("dve", m2, mu, mu, ALU.mult)
            P.tt("dve", rs, rs, m2, ALU.subtract)
            P.act(rs, rs, AF.Ln, bias=EPS)
            P.act(rs, rs, AF.Exp, scale=-0.5)
            for oc in range(8):
                t = TT[:, 3 + oc % 2, 0:n]
                xs = X[:, oc, c0:c0 + n]
                P.tt("dve", t, xs, mu, ALU.subtract)
                P.tt("dve", t, t, rs, ALU.mult)
                P.act(xs, t, AF.Identity, bias=LNB[:, oc, l:l + 1], scale=LNG[:, oc, l:l + 1])
                P.copy("dve", XB[:, oc, c0:c0 + n], xs)

    def out_proj_acc(wsrc_rows, Ybuf, ncol_tiles, first):
        P.dma("pool", WOUT[:, :, :], wsrc_rows.rearrange("(k p) n -> p k n", p=128))
        for oc in range(8):
            for (c0, n, y0) in ncol_tiles:
                pb = nb()
                for k in range(4):
                    P.mm(pb[:, 0:n], WOUT[:, k, oc * 128:(oc + 1) * 128], Ybuf[:, k, y0:y0 + n],
                         start=(k == 0), stop=(k == 3))
                xs = X[:, oc, c0:c0 + n]
                if first:
                    P.stt("dve", xs, xs, ALPHA, pb[:, 0:n], ALU.mult, ALU.add)
                else:
                    P.tt("dve", xs, xs, pb[:, 0:n], ALU.add)

    def run_streams(gens):
        gens = list(gens)
        while gens:
            for g in list(gens):
                try:
                    next(g)
                except StopIteration:
                    gens.remove(g)

    def att_stream(sid, segs):
        ps_s, ps_r, ps_o = PB[3 * sid], PB[3 * sid + 1], PB[3 * sid + 2]
        Es = [TT[:, 4 * sid + k, :] for k in range(3)]
        ER = TT[:, 4 * sid + 3, :]
        Lbs = [LB[3 * sid + k] for k in range(3)]
        Ab = AB[sid]
        items = []
        for sg in segs:
            nblk = len(sg["blocks"])
            for j, blk in enumerate(sg["blocks"]):
                items.append((sg, blk, j, j == 0, j == nblk - 1))
        n = len(items)

        def do_qk(k):
            sg, (nk, c0, mfn, bi), j, first, last = items[k]
            if first and sg.get("pre") is not None:
                sg["pre"]()
            sg["qk"](ps_s, bi, nk, c0, j)
        do_qk(0)
        yield
        for i in range(n + 1):
            if i < n:
                sg, (nk, c0, mfn, bi), j, first, last = items[i]
                NQ = sg["NQ"]
                E = Es[i % 3]
                P.act(E[0:nk, c0:NQ], ps_s[0:nk, c0:NQ], AF.Exp)
                if mfn is not None:
                    mfn(E, nk, c0)
                P.act(Lbs[i % 3][0:nk, c0:NQ], E[0:nk, c0:NQ], AF.Ln, bias=1.0)
            yield
            if i >= 1:
                sg, (nk, c0, mfn, bi), j, first, last = items[i - 1]
                NQ = sg["NQ"]
                Lb = Lbs[(i - 1) % 3]
                P.mm(ps_r[:, c0:NQ], NTm[0:nk, :], Lb[0:nk, c0:NQ], start=first, stop=False, skip_group_check=True)
            yield
            if i + 1 < n:
                do_qk(i + 1)
            yield
            if i >= 1:
                P.act(ER[0:nk, c0:NQ], ps_r[0:nk, c0:NQ], AF.Exp)
            yield
            if i >= 1 and not last:
                P.mm(ps_r[:, c0:NQ], NCm[0:nk, :], Lb[0:nk, c0:NQ], start=False, stop=False, skip_group_check=True)
            yield
            if i >= 1:
                E = Es[(i - 1) % 3]
                P.tt("dve", Ab[0:nk, c0:NQ], ER[0:nk, c0:NQ], E[0:nk, c0:NQ], ALU.mult)
                sg["av"](ps_o, Ab, bi, nk, c0, j, first)
                if last:
                    sg["fin"](ps_o)
            yield

    def even_layer(l):
        i = l // 2
        wsrc = w_in_even[i, :, :]
        for g in range(2):
            ld_w(WIN[0], 0, wsrc, (4 * g) * 128)
            ld_w(WIN[0], 1, wsrc, 2048 + (4 * g) * 128)
            for cc in range(4):
                c = 4 * g + cc
                ld_w(WIN[1], 0, wsrc, 1024 + c * 128)
                ld_w(WIN[1], 1, wsrc, 3072 + c * 128)
                bcast_diag(3, WA, ((i * 8 + c) * 3))
                P.copy("dve", UB[:, 0:2], SA[:, i, c, :])
                P.memset("dve", UB[:, 18:20], 0.0)
                for ti, (c0, n) in enumerate(TILES):
                    ps_h = nb()
                    ps_gc = nb()
                    proj_fm(ps_h, WIN[0], 0, c0, n)
                    proj_fm(ps_gc, WIN[0], 128, c0, n)
                    hs = TT[:, ti % 2, 0:n]
                    P.act(hs, ps_h[:, 0:n], AF.Copy)
                    if ti == 0:
                        P.tt("dve", UB[:, 2:18], ps_gc[:, 0:16], hs[:, 0:16], ALU.mult)
                        P.tt("dve", UB[:, 20:36], ps_gc[:, 16:32], hs[:, 16:32], ALU.mult)
                        P.tt("dve", USTA[:, c, 0:2], ps_gc[:, 14:16], hs[:, 14:16], ALU.mult)
                    else:
                        P.tt("dve", UB[:, c0 + 4:c0 + 4 + n], ps_gc[:, 0:n], hs, ALU.mult)
                    if ti == 4:
                        P.tt("dve", USTA[:, c, 2:4], ps_gc[:, 510:512], hs[:, 510:512], ALU.mult)
                    ps_gb = nb()
                    ps_za = nb()
                    proj_fm(ps_gb, WIN[1], 0, c0, n)
                    proj_fm(ps_za, WIN[1], 128, c0, n)
                    ps_cv = nb()
                    if ti == 0:
                        for j in range(3):
                            P.mm(ps_cv[:, 0:16], DG[:, j, :], UB[:, j:j + 16], start=(j == 0), stop=(j == 2))
                        for j in range(3):
                            P.mm(ps_cv[:, 16:32], DG[:, j, :], UB[:, 18 + j:18 + j + 16], start=(j == 0), stop=(j == 2),
                                 skip_group_check=True)
                    else:
                        for j in range(3):
                            P.mm(ps_cv[:, 0:n], DG[:, j, :], UB[:, c0 + 2 + j:c0 + 2 + j + n], start=(j == 0), stop=(j == 2))
                    sz = TT[:, 2 + 2 * (ti % 2), 0:n]
                    g2 = TT[:, 3 + 2 * (ti % 2), 0:n]
                    P.act(sz, ps_za[:, 0:n], AF.Silu)
                    P.tt("dve", g2, ps_gb[:, 0:n], sz, ALU.mult)
                    P.tt("dve", Y4[:, cc, c0:c0 + n], ps_cv[:, 0:n], g2, ALU.mult)
                if cc < 3:
                    ld_w(WIN[0], 0, wsrc, (c + 1) * 128)
                    ld_w(WIN[0], 1, wsrc, 2048 + (c + 1) * 128)
            out_proj_acc(w_out_even[i, (4 * g) * 128:(4 * g + 4) * 128, :], Y4,
                         [(c0, n, c0) for (c0, n) in TILES], first=(g == 0))
        if CFG["sub"] < 1:
            return
        store_T(lambda c: USTA[:, c, 2:4], 2, 1024, cap_o[i, :, :])
        store_T(lambda c: USTA[:, c, 0:2], 2, 1024, cas_o[i, :, :])

        tblocks = [(0, 16), (16, 16)] + [(PO + 128 * j, 128) for j in range(16)]
        if CFG["sub"] < 2:
            return
        for g in range(2):
            for cc in range(4):
                c = 4 * g + cc
                ld_w(WIN[0], 0, wsrc, 4096 + c * 128)
                ld_w(WIN[0], 1, wsrc, 5120 + c * 128)
                ld_w(WIN[1], 0, wsrc, 6144 + c * 128)
                ld_w(WIN[1], 1, wsrc, 7168 + c * 128)
                for (c0, n) in TILES:
                    ps_q = nb()
                    ps_k = nb()
                    proj_fm(ps_q, WIN[0], 0, c0, n)
                    proj_fm(ps_k, WIN[0], 128, c0, n)
                    P.act(QT[:, c0:c0 + n], ps_q[:, 0:n], AF.Copy, scale=0.125)
                    P.copy("dve", KT[:, c0:c0 + n], ps_k[:, 0:n])
                P.copy("dve", QS[0:64, 2 * cc, :], QT[0:64, 0:16])
                P.copy("dve", QS[64:128, 2 * cc + 1, :], QT[64:128, 0:16])
                P.copy("dve", KS[:, cc, :], KT[:, 0:16])
                for which in range(2):
                    wbuf, wc0 = (WIN[0], 128) if which == 0 else (WIN[1], 0)
                    o_p = kp_o if which == 0 else vp_o
                    o_s = ks_o if which == 0 else vs_o
                    for grp in range(5):
                        tbs = [0, 1] if grp == 0 else [2 + 4 * (grp - 1) + k for k in range(4)]
                        pb = nb()
                        for bi2, tb in enumerate(tbs):
                            t0, nt = tblocks[tb]
                            for k in range(8):
                                P.mm(pb[0:nt, bi2 * 128:(bi2 + 1) * 128], XB[:, k, t0:t0 + nt],
                                     wbuf[:, k, wc0:wc0 + 128], start=(k == 0), stop=(k == 7), skip_group_check=True)
                        stg = TT[:, 6 + (grp % 2), :]
                        if grp == 0:
                            P.copy("dve", stg[0:16, 0:256], pb[0:16, 0:256])
                            for hh in range(2):
                                h = 2 * c + hh
                                P.dma("sp", o_s[i, h, :, :], stg[0:16, hh * 64:(hh + 1) * 64])
                                P.dma("sp", o_p[i, h, 0:16, :], stg[0:16, 128 + hh * 64:128 + (hh + 1) * 64])
                            if which == 1:
                                P.copy("dve", VB[0:16, 0, :], stg[0:16, 0:128])
                                P.copy("dve", VB[0:16, 1, :], stg[0:16, 128:256])
                                P.copy("dve", VS[0:16, cc, :], stg[0:16, 0:128])
                        else:
                            P.copy("dve", stg[:, :], pb[:, :])
                            q0 = 16 + 512 * (grp - 1)
                            for hh in range(2):
                                h = 2 * c + hh
                                P.dma("sp", o_p[i, h, q0:q0 + 512, :].rearrange("(j p) d -> p j d", p=128),
                                      stg.rearrange("p (j f) -> p j f", j=4)[:, :, hh * 64:(hh + 1) * 64])
                            if which == 1:
                                tb0 = 2 + 4 * (grp - 1)
                                P.copy("dve", VB[:, tb0:tb0 + 4, :], stg.rearrange("p (j f) -> p j f", j=4))
                SZ0 = SZ0T[:, :]

                def pre0(cc=cc):
                    pz = PB[6]
                    proj_fm(pz, WIN[1], 128, 0, 32)
                    P.act(SZ0, pz[:, 0:32], AF.Silu)
                    P.copy("dve", SZS[:, cc, :], SZ0[:, 0:16])

                def mk_stream(hh, cc=cc):
                    rows = slice(hh * 64, (hh + 1) * 64)
                    segs = []

                    def qk_m(ps_s, bi, nk, c0, j):
                        P.mm(ps_s[0:16, 0:16], KT[rows, 16:32], QT[rows, 16:32], start=True, stop=True)

                    def av_m(ps_o, Ab, bi, nk, c0, j, first):
                        P.mm(ps_o[:, 0:16], VB[0:16, 1, :], Ab[0:16, 0:16], start=first, stop=False, skip_group_check=True)

                    def mfn_m(E, nk, c0):
                        P.tt("dve", E[0:16, 0:16], E[0:16, 0:16], M0[0:16, 0:16], ALU.mult)

                    def fin_m(ps_o):
                        P.tt("dve", Y4[rows, cc, 16:32], ps_o[rows, 0:16], SZ0[rows, 16:32], ALU.mult)
                    segs.append(dict(NQ=16, blocks=[(16, 0, mfn_m, "M")], qk=qk_m, av=av_m, fin=fin_m,
                                     pre=(pre0 if hh == 0 else None)))
                    for qi in range(4):
                        qc0 = PO + 512 * qi
                        SZ = SZT[:, qi % 2, :]

                        def pre(qc0=qc0, SZ=SZ):
                            pz = PB[6]
                            proj_fm(pz, WIN[1], 128, qc0, 512)
                            P.act(SZ, pz[:, :], AF.Silu)

                        def mfn(E, nk, c0):
                            P.tt("dve", E[:, c0:512], E[:, c0:512], M0[:, 0:512 - c0], ALU.mult)
                        blocks = []
                        for k in (3, 2, 1, 0):
                            blocks.append((128, 128 * k, mfn, 4 * qi + k))
                        for j in range(4 * qi - 1, -1, -1):
                            blocks.append((128, 0, None, j))
                        blocks.append((16, 0, None, "M"))

                        def qk(ps_s, bi, nk, c0, j, qc0=qc0):
                            kc0 = 16 if bi == "M" else PO + 128 * bi
                            P.mm(ps_s[0:nk, c0:512], KT[rows, kc0:kc0 + nk], QT[rows, qc0 + c0:qc0 + 512],
                                 start=True, stop=True)

                        def av(ps_o, Ab, bi, nk, c0, j, first):
                            tb = 1 if bi == "M" else 2 + bi
                            P.mm(ps_o[:, c0:512], VB[0:nk, tb, :], Ab[0:nk, c0:512], start=first, stop=False,
                                 skip_group_check=True)

                        def fin(ps_o, qc0=qc0, SZ=SZ):
                            P.tt("dve", Y4[rows, cc, qc0:qc0 + 512], ps_o[rows, :], SZ[rows, :], ALU.mult)
                        segs.append(dict(NQ=512, blocks=blocks, qk=qk, av=av, fin=fin, pre=(pre if hh == 0 else None)))
                    return att_stream(hh, segs)
                run_streams([mk_stream(0), mk_stream(1)])

            def sample_stream():
                NQ = 128
                blocks = [(16, 0, "S", "S")] + [(16 if b == 32 else 128, 0, None, b) for b in range(32, -1, -1)][:CFG["sblk"]]

                def qk(ps_s, bi, nk, c0, j):
                    if bi == "S":
                        for hq in range(8):
                            cq, hh = hq // 2, hq % 2
                            P.mm(ps_s[0:16, hq * 16:(hq + 1) * 16], KS[:, cq, :], QS[:, hq, :], start=True, stop=True,
                                 skip_group_check=True)
                        return
                    r0 = 128 * bi

                    def kdma(jj):
                        nkk, _, _, bb = blocks[jj]
                        P.dma("sp", KC[jj % 2][0:nkk, :].rearrange("s (h d) -> s h d", h=8),
                              ck[i, 8 * g:8 * g + 8, 128 * bb:128 * bb + nkk, :].rearrange("h s d -> s h d"))
                    if j == 1:
                        kdma(1)
                    P.dma("pool", VC[j % 4][0:nk, :].rearrange("s (h d) -> s h d", h=8),
                          cv[i, 8 * g:8 * g + 8, r0:r0 + nk, :].rearrange("h s d -> s h d"))
                    pt = PB[7]
                    for cq in range(4):
                        P.transpose(pt[:, cq * 128:cq * 128 + nk], KC[j % 2][0:nk, cq * 128:(cq + 1) * 128], IDF[0:nk, 0:nk])
                    P.copy("dve", KTC[:, :, 0:nk], pt[:, :].rearrange("p (c s) -> p c s", c=4)[:, :, 0:nk])
                    if j + 1 < len(blocks):
                        kdma(j + 1)
                    for hq in range(8):
                        cq, hh = hq // 2, hq % 2
                        P.mm(ps_s[0:nk, hq * 16:(hq + 1) * 16], KTC[:, cq, 0:nk], QS[:, hq, :], start=True, stop=True,
                             skip_group_check=True)

                def av(ps_o, Ab, bi, nk, c0, j, first):
                    for hq in range(8):
                        cq = hq // 2
                        if bi == "S":
                            lhs = VS[0:16, cq, :]
                        else:
                            lhs = VC[j % 4][0:nk, cq * 128:(cq + 1) * 128]
                        P.mm(ps_o[:, hq * 16:(hq + 1) * 16], lhs, Ab[0:nk, hq * 16:(hq + 1) * 16],
                             start=(first and hq == 0), stop=False, skip_group_check=True)

                def mfn(E, nk, c0):
                    m = bass.AP(M0, 0, [[512, 16], [0, 8], [1, 16]])
                    ev = E[0:16, 0:128].rearrange("p (h t) -> p h t", h=8)
                    P.tt("dve", ev, ev, m, ALU.mult)

                def fin(ps_o):
                    for hq in range(8):
                        cq, hh = hq // 2, hq % 2
                        rows = slice(hh * 64, (hh + 1) * 64)
                        P.tt("dve", Y4[rows, cq, 0:16], ps_o[rows, hq * 16:(hq + 1) * 16], SZS[rows, cq, :], ALU.mult)
                blocks = [(nk, c0, (mfn if m else None), bi) for (nk, c0, m, bi) in blocks]
                return att_stream(0, [dict(NQ=NQ, blocks=blocks, qk=qk, av=av, fin=fin, pre=None)])
            if CFG["sub"] < 5:
                continue
            run_streams([sample_stream()])
            if CFG["sub"] < 6:
                continue
            out_proj_acc(w_out_even[i, 1024 + (4 * g) * 128:1024 + (4 * g + 4) * 128, :], Y4,
                         [(c0, n, c0) for (c0, n) in TILES], first=False)
        if CFG["sub"] < 7:
            return
        post_ln(TILES, l)

    def odd_layer(l):
        i = l // 2
        wsrc = w_in_odd[i, :, :]
        groups = [[TILES[0], TILES[1]], [TILES[2]], [TILES[3]], [TILES[4]]]
        for gi, gt in enumerate(groups):
            gc0 = gt[0][0]
            ncols = sum(n for _, n in gt)
            nst = len(gt)
            st["reserved"] = set()
            ps_sum = [PB[0], PB[1]]
            ps_sq = [PB[2], PB[1]]
            st["reserved"] = {0, 1, 2}
            ld_w(WIN[0], 0, wsrc, 0)
            ld_w(WIN[0], 1, wsrc, 2048)
            ld_w(WIN[1], 0, wsrc, 128)
            ld_w(WIN[1], 1, wsrc, 2048 + 128)
            items = [(c, ti) for c in range(16) for ti in range(nst)]
            pend = {}

            def proj_item(k):
                c, ti = items[k]
                c0, n = gt[ti]
                ps_a = nb()
                ps_g = nb()
                proj_fm(ps_a, WIN[c % 2], 0, c0, n)
                proj_fm(ps_g, WIN[c % 2], 128, c0, n)
                pend[k] = (ps_a, ps_g)
                if ti == nst - 1 and c + 2 < 16:
                    ld_w(WIN[c % 2], 0, wsrc, (c + 2) * 128)
                    ld_w(WIN[c % 2], 1, wsrc, 2048 + (c + 2) * 128)
            proj_item(0)
            for k, (c, ti) in enumerate(items):
                if k + 1 < len(items):
                    proj_item(k + 1)
                c0, n = gt[ti]
                ps_a, ps_g = pend.pop(k)
                if ti == 0:
                    bcast_diag(31, WC, ((i * 16 + c) * 31))
                    if gi == 0:
                        P.copy("dve", UB[:, 0:30], SC[:, i, c, :])
                        P.memset("dve", UB[:, 64:94], 0.0)
                    else:
                        P.copy("dve", UB[:, 64:94], HIST[:, c, :])
                sg = TT[:, k % 4, 0:n]
                P.act(sg, ps_g[:, 0:n], AF.Sigmoid)
                lc0 = c0 - gc0
                if gi == 0 and ti == 0:
                    P.tt("dve", UB[:, 30:46], ps_a[:, 0:16], sg[:, 0:16], ALU.mult)
                    P.tt("dve", UB[:, 94:110], ps_a[:, 16:32], sg[:, 16:32], ALU.mult)
                    P.tt("dve", USTS[:, c, :], ps_a[:, 0:16], sg[:, 0:16], ALU.mult)
                    ub0 = None
                else:
                    ub0 = 94 + (lc0 - 16 if gi == 0 else lc0)
                    P.tt("dve", UB[:, ub0:ub0 + n], ps_a[:, 0:n], sg, ALU.mult)
                if gi == 3:
                    P.tt("dve", USTP[:, c, :], ps_a[:, 482:512], sg[:, 482:512], ALU.mult)
                ps_cv = nb()
                if ub0 is None:
                    for j in range(31):
                        P.mm(ps_cv[:, 0:16], DG[:, j, :], UB[:, j:j + 16], start=(j == 0), stop=(j == 30))
                    for j in range(31):
                        P.mm(ps_cv[:, 16:32], DG[:, j, :], UB[:, 64 + j:64 + j + 16], start=(j == 0), stop=(j == 30),
                             skip_group_check=True)
                else:
                    for j in range(31):
                        P.mm(ps_cv[:, 0:n], DG[:, j, :], UB[:, ub0 - 30 + j:ub0 - 30 + j + n],
                             start=(j == 0), stop=(j == 30))
                cs = CC[:, c, lc0:lc0 + n]
                P.act(cs, ps_cv[:, 0:n], AF.Identity, bias=CVB[:, c, i:i + 1])
                cb = LB[k % 2][:, 0:n]
                cq = AB[k % 2][:, 0:n]
                P.act(cb, cs, AF.Copy)
                P.act(cq, cs, AF.Square)
                if nst == 2 and ti == 0:
                    P.mm(PB[1][:, 0:32], ONES[:, :], cb, start=(c == 0), stop=(c == 15), skip_group_check=True)
                    P.mm(PB[1][:, 32:64], ONES[:, :], cq, start=False, stop=(c == 15), skip_group_check=True)
                else:
                    P.mm(PB[0][:, 0:n], ONES[:, :], cb, start=(c == 0), stop=(c == 15), skip_group_check=True)
                    P.mm(PB[2][:, 0:n], ONES[:, :], cq, start=(c == 0), stop=(c == 15), skip_group_check=True)
                if gi < 3 and ti == nst - 1:
                    lastcol = 94 + (ncols - 16 if gi == 0 else ncols)
                    P.copy("dve", HIST[:, c, :], UB[:, lastcol - 30:lastcol])
            stats = []
            for ti, (c0, n) in enumerate(gt):
                if n == 32:
                    mu, rs = ST0[:, 0, :], ST0[:, 1, :]
                else:
                    mu, rs = SZT[:, 0, 0:n], SZT[:, 1, 0:n]
                if nst == 2 and ti == 0:
                    s_ap, q_ap = PB[1][:, 0:32], PB[1][:, 32:64]
                else:
                    s_ap, q_ap = PB[0][:, 0:n], PB[2][:, 0:n]
                P.act(mu, s_ap, AF.Copy, scale=1.0 / 2048)
                P.act(rs, q_ap, AF.Copy, scale=1.0 / 2048)
                m2 = TT[:, 0, 0:n]
                P.tt("dve", m2, mu, mu, ALU.mult)
                P.tt("dve", rs, rs, m2, ALU.subtract)
                P.act(rs, rs, AF.Ln, bias=EPS)
                P.act(rs, rs, AF.Exp, scale=-0.5)
                stats.append((mu, rs))
            st["reserved"] = set()
            ld_w(WIN[0], 0, wsrc, 4096)
            ld_w(WIN[0], 1, wsrc, 4096 + 128)

            def zproj(c):
                wb = WIN[(c // 2) % 2]
                if c % 2 == 0 and c < 14:
                    ld_w(WIN[(c // 2 + 1) % 2], 0, wsrc, 4096 + (c + 2) * 128)
                    ld_w(WIN[(c // 2 + 1) % 2], 1, wsrc, 4096 + (c + 3) * 128)
                res = []
                for ti, (c0, n) in enumerate(gt):
                    pz = PB[(c % 2) * 2 + ti]
                    proj_fm(pz, wb, (c % 2) * 128, c0, n)
                    res.append(pz)
                return res
            st["reserved"] = {0, 1, 2, 3}
            pzs_next = zproj(0)
            for c in range(16):
                pzs = pzs_next
                if c + 1 < 16:
                    pzs_next = zproj(c + 1)
                for ti, (c0, n) in enumerate(gt):
                    lc0 = c0 - gc0
                    mu, rs = stats[ti]
                    pz = pzs[ti]
                    sz = TT[:, 3 * (c % 2), 0:n]
                    P.act(sz, pz[:, 0:n], AF.Silu)
                    t = TT[:, 3 * (c % 2) + 1, 0:n]
                    cs = CC[:, c, lc0:lc0 + n]
                    P.tt("dve", t, cs, mu, ALU.subtract)
                    P.tt("dve", t, t, rs, ALU.mult)
                    s2 = TT[:, 3 * (c % 2) + 2, 0:n]
                    P.act(s2, t, AF.Silu, bias=LCB[:, c, i:i + 1], scale=LCG[:, c, i:i + 1])
                    P.tt("dve", YO[:, c % 4, lc0:lc0 + n], s2, sz, ALU.mult)
                if c % 4 == 3:
                    k0 = (c // 4) * 4
                    out_proj_acc(w_out_odd[i, k0 * 128:(k0 + 4) * 128, :], YO,
                                 [(c0, n, c0 - gc0) for (c0, n) in gt], first=(c == 3))
            st["reserved"] = set()
            post_ln(gt, l)
        store_T(lambda c: USTP[:, c, :], 30, 2048, ccp_o[i, :, :])
        P.dma("sp", ccs_o[i, 0:14, :], scc[i, 16:30, :])
        store_T(lambda c: USTS[:, c, :], 16, 2048, ccs_o[i, 14:30, :])

    for l in range(CFG["layers"]):
        if l % 2 == 0:
            even_layer(l)
        else:
            odd_layer(l)

    for bi, (r0, nr) in enumerate(blocks_in if CFG["pre"] & 8 else []):
        for g in range(2):
            pb = nb()
            for cc in range(4):
                P.transpose(pb[0:nr, cc * 128:(cc + 1) * 128], X[:, 4 * g + cc, r0:r0 + nr], IDF[:, :])
            P.copy("dve", TT[0:nr, (bi % 2) * 2 + g, :], pb[0:nr, :])
        P.dma("sp", y_o[r0:r0 + nr, :].rearrange("r (a b) -> r a b", b=512), TT[0:nr, (bi % 2) * 2:(bi % 2) * 2 + 2, :])

    P.finish()
    return nc


_NC_CACHE = {}


def kernel(x_prompt, x_sample, cache_sb_k, cache_sb_v, state_conv_a, state_conv_c, meta_tokens,
           w_in_even, conv_a_w, w_out_even, w_in_odd, conv_c_w, conv_c_b, ln_c_g, ln_c_b,
           w_out_odd, post_ln_g, post_ln_b):
    f = lambda a: np.ascontiguousarray(np.asarray(a, dtype=np.float32))
    x_prompt, x_sample, cache_sb_k, cache_sb_v = f(x_prompt), f(x_sample), f(cache_sb_k), f(cache_sb_v)
    state_conv_a, state_conv_c, meta_tokens = f(state_conv_a), f(state_conv_c), f(meta_tokens)
    shared = {"w_in_even": f(w_in_even), "conv_a_w": f(conv_a_w), "w_out_even": f(w_out_even),
              "w_in_odd": f(w_in_odd), "conv_c_w": f(conv_c_w), "conv_c_b": f(conv_c_b), "ln_c_g": f(ln_c_g),
              "ln_c_b": f(ln_c_b), "w_out_odd": f(w_out_odd), "post_ln_g": f(post_ln_g), "post_ln_b": f(post_ln_b)}
    if "nc" not in _NC_CACHE:
        _NC_CACHE["nc"] = build_program()
    nc = _NC_CACHE["nc"]
    in_maps = []
    for b in range(NCORES):
        m = dict(shared)
        m["xin"] = np.ascontiguousarray(np.concatenate([x_sample[b], meta_tokens, x_prompt[b]], axis=0))
        m["ck"] = np.ascontiguousarray(cache_sb_k[:, b])
        m["cv"] = np.ascontiguousarray(cache_sb_v[:, b])
        m["sca"] = np.ascontiguousarray(state_conv_a[:, b])
        m["scc"] = np.ascontiguousarray(state_conv_c[:, b])
        in_maps.append(m)
    res = run_bass_kernel_spmd(nc, in_maps, core_ids=list(range(NCORES)))
    R = res.results
    y = np.stack([r["y"] for r in R], 0)
    y_prompt = np.ascontiguousarray(y[:, PO:, :])
    y_sample = np.ascontiguousarray(y[:, 0:16, :])
    st = lambda k: np.ascontiguousarray(np.stack([r[k] for r in R], 1)).astype(np.float32)
    return (y_prompt, y_sample, st("kp"), st("vp"), st("cap"), st("ccp"), st("ks"), st("vs"), st("cas"), st("ccs"))
```

```python
import numpy as np
from contextlib import ExitStack
import concourse.bass as bass
import concourse.mybir as mybir
from concourse.bass_utils import run_bass_kernel_spmd

F32 = mybir.dt.float32
BF16 = mybir.dt.bfloat16
AF = mybir.ActivationFunctionType
ALU = mybir.AluOpType

NDMASEM = 12


def _region(ap):
    tn = type(ap.tensor).__name__
    if tn.startswith("DRam"):
        return None
    dims = list(ap.ap)
    pstride, pcnt = dims[0]
    off = ap.offset
    if pstride == 0:
        return (ap.tensor.name, 0, 128, 0, 1 << 40)
    p0 = off // pstride
    f0 = off - p0 * pstride
    ext = 0
    for st, cnt in dims[1:]:
        ext += (cnt - 1) * abs(st)
    es = mybir.dt.size(ap.dtype)
    if tn.startswith("PSum"):
        return (ap.tensor.name, (p0 // 32) * 32, ((p0 + pcnt + 31) // 32) * 32, 0, 1 << 40)
    return (ap.tensor.name, p0, p0 + pcnt, f0 * es, (f0 + ext + 1) * es)


def _ovl(a, b):
    return a[1] < b[2] and b[1] < a[2] and a[3] < b[4] and b[3] < a[4]


def _cov(a, b):
    return a[1] <= b[1] and a[2] >= b[2] and a[3] <= b[3] and a[4] >= b[4]


class _Op:
    __slots__ = ("eng", "fn", "deps", "sig", "sigval", "is_dma", "dsem", "dval", "thr", "gi")


class Prog:
    def __init__(self, nc):
        self.nc = nc
        self.stack = ExitStack()
        self.ops = []
        self.wr = {}
        self.rd = {}
        self.ndma = {"sp": 0, "pool": 0, "act": 0}
        self.dma_sems = {}
        self.nbuf = 0

    def sb(self, name, shape, dtype):
        return self.stack.enter_context(self.nc.sbuf_tensor(name, list(shape), dtype))

    def ps(self, name, shape, dtype):
        return self.stack.enter_context(self.nc.psum_tensor(name, list(shape), dtype))

    def _rec(self, eng, fn, reads, writes, is_dma=False):
        op = _Op()
        op.eng = eng
        op.fn = fn
        op.is_dma = is_dma
        op.sig = False
        op.sigval = 0
        op.dsem = None
        op.dval = 0
        op.thr = None
        gi = len(self.ops)
        op.gi = gi
        deps = set()
        teng = "dma" if is_dma else eng
        writes = list(writes) + [ap for ap in reads if type(ap.tensor).__name__.startswith("PSum")]
        reads = [ap for ap in reads if not type(ap.tensor).__name__.startswith("PSum")]
        for ap in reads:
            r = _region(ap)
            if r is None:
                continue
            for w in self.wr.get(r[0], ()):
                if _ovl(w[0], r):
                    deps.add(w[1])
            lst = self.rd.setdefault(r[0], [])
            if teng != "dma":
                lst[:] = [x for x in lst if not (x[2] == teng and _cov(r, x[0]))]
            lst.append((r, gi, teng))
        for ap in writes:
            r = _region(ap)
            if r is None:
                continue
            wl = self.wr.setdefault(r[0], [])
            rl = self.rd.setdefault(r[0], [])
            for w in wl:
                if _ovl(w[0], r):
                    deps.add(w[1])
            for x in rl:
                if _ovl(x[0], r) and x[1] != gi:
                    deps.add(x[1])
            wl[:] = [w for w in wl if not _cov(r, w[0])]
            rl[:] = [x for x in rl if not _cov(r, x[0]) or x[1] == gi]
            wl.append((r, gi))
        deps.discard(gi)
        op.deps = deps
        self.ops.append(op)
        return op

    def mm(self, out, lhsT, rhs, start=True, stop=True, **kw):
        rd = [lhsT, rhs] if start else [lhsT, rhs, out]
        return self._rec("pe", lambda e: e.matmul(out, lhsT, rhs, start=start, stop=stop, **kw), rd, [out])

    def transpose(self, out, in_, ident):
        return self._rec("pe", lambda e: e.transpose(out, in_, ident), [in_, ident], [out])

    def act(self, out, in_, func, bias=0.0, scale=1.0, eng="act"):
        rd = [in_]
        if not isinstance(bias, (int, float)):
            rd.append(bias)
        if not isinstance(scale, (int, float)):
            rd.append(scale)
        return self._rec(eng, lambda e: e.activation(out, in_, func, bias=bias, scale=scale), rd, [out])

    def tt(self, eng, out, in0, in1, op):
        return self._rec(eng, lambda e: e.tensor_tensor(out, in0, in1, op), [in0, in1], [out])

    def ts(self, eng, out, in0, s1, s2, op0, op1=None):
        rd = [in0]
        if not isinstance(s1, (int, float)):
            rd.append(s1)
        if s2 is not None and not isinstance(s2, (int, float)):
            rd.append(s2)
        if op1 is None:
            return self._rec(eng, lambda e: e.tensor_scalar(out, in0, s1, None, op0), rd, [out])
        return self._rec(eng, lambda e: e.tensor_scalar(out, in0, s1, s2, op0, op1), rd, [out])

    def stt(self, eng, out, in0, scalar, in1, op0, op1):
        rd = [in0, in1]
        if not isinstance(scalar, (int, float)):
            rd.append(scalar)
        return self._rec(eng, lambda e: e.scalar_tensor_tensor(out, in0, scalar, in1, op0, op1), rd, [out])

    def copy(self, eng, out, in_):
        if eng == "act":
            return self._rec(eng, lambda e: e.copy(out, in_), [in_], [out])
        return self._rec(eng, lambda e: e.tensor_copy(out, in_), [in_], [out])

    def recip(self, eng, out, in_):
        return self._rec(eng, lambda e: e.reciprocal(out, in_), [in_], [out])

    def memset(self, eng, ap, val):
        return self._rec(eng, lambda e: e.memset(ap, val), [], [ap])

    def custom(self, eng, fn, reads, writes):
        return self._rec(eng, fn, reads, writes)

    def dma(self, q, out, in_):
        op = self._rec(q, lambda e: e.dma_start(out=out, in_=in_), [in_], [out], is_dma=True)
        i = self.ndma[q]
        self.ndma[q] = i + 1
        op.dsem = (q, i % NDMASEM)
        op.dval = 16 * (i // NDMASEM + 1)
        if i >= NDMASEM:
            op.thr = ((q, i % NDMASEM), 16 * (i // NDMASEM))
        return op

    def finish(self):
        nc = self.nc
        ops = self.ops
        engs = ["pe", "act", "dve", "pool", "sp"]
        for op in ops:
            best = {}
            keep = set()
            for d in op.deps:
                dop = ops[d]
                if dop.is_dma:
                    keep.add(d)
                    continue
                if dop.eng == "pe" and op.eng == "pe" and not op.is_dma:
                    continue
                if d > best.get(dop.eng, -1):
                    best[dop.eng] = d
            for d in best.values():
                ops[d].sig = True
                keep.add(d)
            op.deps = keep
        cnt = {e: 0 for e in engs}
        for op in ops:
            if op.sig and not op.is_dma:
                cnt[op.eng] += 1
                op.sigval = cnt[op.eng]
        esem = {e: self.stack.enter_context(nc.semaphore("s_" + e)) for e in engs}
        dsem = {}
        for q in ("sp", "pool", "act"):
            for k in range(min(NDMASEM, self.ndma[q])):
                dsem[(q, k)] = self.stack.enter_context(nc.semaphore("d_%s_%d" % (q, k)))
        per = {e: [] for e in engs}
        for op in ops:
            per[op.eng].append(op)
        dfinal = {}
        for op in ops:
            if op.is_dma:
                dfinal[op.dsem] = max(dfinal.get(op.dsem, 0), op.dval)

        def emit(ename, e):
            waited = {}

            def wait(key, sem, val):
                if waited.get(key, 0) >= val:
                    return
                waited[key] = val
                e.wait_ge(sem, val)

            for op in per[ename]:
                for d in sorted(op.deps):
                    dop = ops[d]
                    if dop.is_dma:
                        wait(dop.dsem, dsem[dop.dsem], dop.dval)
                    else:
                        if dop.eng == "pe" and ename == "pe" and not op.is_dma:
                            continue
                        wait(dop.eng, esem[dop.eng], dop.sigval)
                if op.is_dma:
                    if op.thr is not None:
                        wait(op.thr[0], dsem[op.thr[0]], op.thr[1])
                    op.fn(e).then_inc(dsem[op.dsem], 16)
                else:
                    ins = op.fn(e)
                    if op.sig:
                        ins.then_inc(esem[ename], 1)
            for key, val in dfinal.items():
                if key[0] == ename:
                    wait(key, dsem[key], val)

        with nc.allow_non_contiguous_dma(reason="small strided state/param transfers"), nc.Block() as block:
            @block.tensor
            def _(e):
                emit("pe", e)

            @block.scalar
            def _(e):
                emit("act", e)

            @block.vector
            def _(e):
                emit("dve", e)

            @block.gpsimd
            def _(e):
                emit("pool", e)

            @block.sync
            def _(e):
                emit("sp", e)
        self.stack.close()
        return cnt


NCORES = 8
D = 1024
T = 2080
PO = 32
TILES = [(0, 32), (32, 512), (544, 512), (1056, 512), (1568, 512)]
ALPHA = float(8 ** 0.25)
EPS = 1e-5
PAST = 4112


CFG = {"layers": 4, "sub": 99, "pre": 15, "sblk": 33, "sdbg": 99}


def build_program():
    nc = bass.Bass("TRN2", target_bir_lowering=False)

    def din(name, shape):
        return nc.dram_tensor(name, list(shape), F32, kind="ExternalInput")

    def dout(name, shape):
        return nc.dram_tensor(name, list(shape), F32, kind="ExternalOutput")

    xin = din("xin", [T, D])
    ck = din("ck", [2, 16, PAST, 64])
    cv = din("cv", [2, 16, PAST, 64])
    sca = din("sca", [2, 2, 1024])
    scc = din("scc", [2, 30, 2048])
    w_in_even = din("w_in_even", [2, 1024, 8192])
    conv_a_w = din("conv_a_w", [2, 3, 1024])
    w_out_even = din("w_out_even", [2, 2048, 1024])
    w_in_odd = din("w_in_odd", [2, 1024, 6144])
    conv_c_w = din("conv_c_w", [2, 31, 2048])
    conv_c_b = din("conv_c_b", [2, 2048])
    ln_c_g = din("ln_c_g", [2, 2048])
    ln_c_b = din("ln_c_b", [2, 2048])
    w_out_odd = din("w_out_odd", [2, 2048, 1024])
    post_ln_g = din("post_ln_g", [4, 1024])
    post_ln_b = din("post_ln_b", [4, 1024])
    y_o = dout("y", [T, D])
    kp_o = dout("kp", [2, 16, 2064, 64])
    vp_o = dout("vp", [2, 16, 2064, 64])
    ks_o = dout("ks", [2, 16, 16, 64])
    vs_o = dout("vs", [2, 16, 16, 64])
    cap_o = dout("cap", [2, 2, 1024])
    ccp_o = dout("ccp", [2, 30, 2048])
    cas_o = dout("cas", [2, 2, 1024])
    ccs_o = dout("ccs", [2, 30, 2048])

    P = Prog(nc)
    X = P.sb("X", [128, 8, T], F32)
    XB = P.sb("XB", [128, 8, T], BF16)
    WIN = [P.sb("WIN%d" % k, [128, 8, 256], BF16) for k in range(2)]
    WOUT = P.sb("WOUT", [128, 4, 1024], BF16)
    AR = P.sb("AR", [128, 9792], F32)
    ARb = AR.bitcast(BF16)
    TT = P.sb("TT", [128, 8, 512], F32)
    LB = [P.sb("LB%d" % k, [128, 512], BF16) for k in range(6)]
    SZT = P.sb("SZT", [128, 2, 512], F32)
    SZ0T = P.sb("SZ0T", [128, 32], F32)
    ST0 = P.sb("ST0", [128, 2, 32], F32)
    AB = [P.sb("AB%d" % k, [128, 512], BF16) for k in range(2)]
    DU = P.sb("DU", [128, 6058], BF16)
    DG = DU[:, 0:3968].rearrange("p (j q) -> p j q", j=31)
    UB = DU[:, 3968:6058]
    SZA = DU.bitcast(F32)[:, 0:2080]
    IDF = P.sb("IDF", [128, 128], F32)
    IDB = P.sb("IDB", [128, 128], BF16)
    NTm = P.sb("NTm", [128, 128], BF16)
    NCm = P.sb("NCm", [128, 128], BF16)
    M0 = P.sb("M0", [128, 512], F32)
    ONES = P.sb("ONES", [128, 128], BF16)
    QS = P.sb("QS", [128, 8, 16], BF16)
    KS = P.sb("KS", [128, 4, 16], BF16)
    VS = P.sb("VS", [16, 4, 128], BF16)
    SZS = P.sb("SZS", [128, 4, 16], F32)
    USTA = P.sb("USTA", [128, 8, 4], F32)
    USTS = P.sb("USTS", [128, 16, 16], F32)
    USTP = P.sb("USTP", [128, 16, 30], F32)
    HIST = P.sb("HIST", [128, 16, 30], BF16)
    LNG = P.sb("LNG", [128, 8, 4], F32)
    LNB = P.sb("LNB", [128, 8, 4], F32)
    LCG = P.sb("LCG", [128, 16, 2], F32)
    LCB = P.sb("LCB", [128, 16, 2], F32)
    CVB = P.sb("CVB", [128, 16, 2], F32)
    WA = P.sb("WA", [128, 2, 8, 3], F32)
    WC = P.sb("WC", [128, 2, 16, 31], BF16)
    SA = P.sb("SA", [128, 2, 8, 2], F32)
    SC = P.sb("SC", [128, 2, 16, 30], BF16)
    PB = [P.ps("pb%d" % k, [128, 512], F32) for k in range(8)]

    QT = ARb[:, 0:2080]
    KT = ARb[:, 2080:4160]
    VB = ARb[:, 4160:6464].rearrange("p (b f) -> p b f", b=18)
    KTC = ARb[:, 6464:6976].rearrange("p (c s) -> p c s", c=4)
    VC = [ARb[:, 6976:7488], ARb[:, 7488:8000], ARb[:, 18368:18880], ARb[:, 18880:19392]]
    KC = [AR[:, 4000:4512], AR[:, 4512:5024]]
    Y4 = ARb[:, 10048:18368].rearrange("p (c t) -> p c t", c=4)
    CC = AR[:, 0:8704].rearrange("p (c t) -> p c t", c=16)
    YO = ARb[:, 17408:19584].rearrange("p (c t) -> p c t", c=4)

    st = {"rot": 0, "reserved": set()}

    def nb():
        while True:
            k = st["rot"] % 8
            st["rot"] += 1
            if k not in st["reserved"]:
                return PB[k]

    P.memset("pool", IDF[:], 1.0)
    P.custom("pool", lambda e: e.affine_select(out=IDF[:], in_=IDF[:], compare_op=ALU.is_ge, fill=0.0, base=0,
                                               pattern=[[-1, 128]], channel_multiplier=1), [IDF[:]], [IDF[:]])
    P.custom("pool", lambda e: e.affine_select(out=IDF[:], in_=IDF[:], compare_op=ALU.is_ge, fill=0.0, base=0,
                                               pattern=[[1, 128]], channel_multiplier=-1), [IDF[:]], [IDF[:]])
    P.copy("pool", IDB[:], IDF[:])
    P.memset("pool", TT[:, 0, 0:128], -1.0)
    P.custom("pool", lambda e: e.affine_select(out=TT[:, 0, 0:128], in_=TT[:, 0, 0:128], compare_op=ALU.is_ge, fill=0.0,
                                               base=0, pattern=[[-1, 128]], channel_multiplier=1),
             [TT[:, 0, 0:128]], [TT[:, 0, 0:128]])
    P.copy("pool", NTm[:], TT[:, 0, 0:128])
    P.memset("pool", TT[:, 1, 0:128], -1.0)
    P.custom("pool", lambda e: e.affine_select(out=TT[:, 1, 0:128], in_=TT[:, 1, 0:128], compare_op=ALU.is_gt, fill=0.0,
                                               base=0, pattern=[[1, 128]], channel_multiplier=-1),
             [TT[:, 1, 0:128]], [TT[:, 1, 0:128]])
    P.copy("pool", NCm[:], TT[:, 1, 0:128])
    P.memset("pool", M0[:], 1.0)
    P.custom("pool", lambda e: e.affine_select(out=M0[:], in_=M0[:], compare_op=ALU.is_gt, fill=0.0, base=0,
                                               pattern=[[1, 512]], channel_multiplier=-1), [M0[:]], [M0[:]])
    P.memset("pool", ONES[:], 1.0)
    P.memset("pool", QS[:], 0.0)

    def load_T(src, R, W, dest):
        nch = W // 128
        stg = TT[0:R, 4:8, :] if W == 2048 else TT[0:R, 4:6, :]
        P.dma("sp", stg, src.rearrange("r (a b) -> r a b", b=512))
        pb = nb()
        for c in range(nch):
            P.transpose(pb[:, c * R:(c + 1) * R], TT[0:R, 4 + c // 4, (c % 4) * 128:(c % 4 + 1) * 128], IDF[0:R, 0:R])
        P.copy("dve", dest, pb[:, 0:nch * R].rearrange("p (c r) -> p c r", r=R))

    def store_T(src_fn, R, W, dst):
        nch = W // 128
        for g in range(nch // 4):
            pb = nb()
            for cc in range(4):
                P.transpose(pb[0:R, cc * 128:(cc + 1) * 128], src_fn(4 * g + cc), IDF[:, :])
            P.copy("dve", TT[0:R, 4 + g, :], pb[0:R, :])
        stg = TT[0:R, 4:4 + nch // 4, :]
        P.dma("sp", dst.rearrange("r (a b) -> r a b", b=512), stg)

    if CFG["pre"] & 2:
      load_T(post_ln_g[:, :], 4, 1024, LNG[:, :, :])
      load_T(post_ln_b[:, :], 4, 1024, LNB[:, :, :])
      load_T(ln_c_g[:, :], 2, 2048, LCG[:, :, :])
      load_T(ln_c_b[:, :], 2, 2048, LCB[:, :, :])
      load_T(conv_c_b[:, :], 2, 2048, CVB[:, :, :])
      for i in range(2):
        load_T(conv_a_w[i, :, :], 3, 1024, WA[:, i, :, :])
        load_T(conv_c_w[i, :, :], 31, 2048, WC[:, i, :, :])
        load_T(sca[i, :, :], 2, 1024, SA[:, i, :, :])
        load_T(scc[i, :, :], 30, 2048, SC[:, i, :, :])

    blocks_in = [(0, 32)] + [(PO + 128 * j, 128) for j in range(16)]
    for bi, (r0, nr) in enumerate(blocks_in if CFG["pre"] & 4 else []):
        stg = TT[0:nr, (bi % 2) * 2:(bi % 2) * 2 + 2, :]
        P.dma("sp", stg, xin[r0:r0 + nr, :].rearrange("r (a b) -> r a b", b=512))
        for g in range(2):
            pb = nb()
            for cc in range(4):
                P.transpose(pb[:, cc * 128:cc * 128 + nr], TT[0:nr, (bi % 2) * 2 + g, cc * 128:(cc + 1) * 128],
                            IDF[0:nr, 0:nr])
            src = pb[:, :].rearrange("p (c t) -> p c t", c=4)[:, :, 0:nr]
            P.copy("dve", X[:, 4 * g:4 * g + 4, r0:r0 + nr], src)
            P.act(XB[:, 4 * g:4 * g + 4, r0:r0 + nr], src, AF.Copy)

    def ld_w(buf, half, wsrc, col0):
        P.dma("pool", buf[:, :, half * 128:(half + 1) * 128],
              wsrc[:, col0:col0 + 128].rearrange("(k p) n -> p k n", p=128))

    def proj_fm(ps, wbuf, wc0, col0, n):
        for k in range(8):
            P.mm(ps[:, 0:n], wbuf[:, k, wc0:wc0 + 128], XB[:, k, col0:col0 + n], start=(k == 0), stop=(k == 7))

    def bcast_diag(ntaps, wap_tensor, woff):
        in0 = bass.AP(IDB, 0, [[128, 128], [0, ntaps], [1, 128]])
        in1 = bass.AP(wap_tensor, woff, [[wap_tensor.shape[1] * (wap_tensor.shape[2] if len(wap_tensor.shape) > 2 else 1)
                                          * (wap_tensor.shape[3] if len(wap_tensor.shape) > 3 else 1), 128],
                                         [1, ntaps], [0, 128]])
        if ntaps > 8:
            h = ntaps // 2
            ps_ = in1.ap[0][0]
            in0a = bass.AP(IDB, 0, [[128, 128], [0, h], [1, 128]])
            in0b = bass.AP(IDB, 0, [[128, 128], [0, ntaps - h], [1, 128]])
            in1a = bass.AP(wap_tensor, woff, [[ps_, 128], [1, h], [0, 128]])
            in1b = bass.AP(wap_tensor, woff + h, [[ps_, 128], [1, ntaps - h], [0, 128]])
            P.tt("dve", DG[:, 0:h, :], in0a, in1a, ALU.mult)
            P.tt("dve", DG[:, h:ntaps, :], in0b, in1b, ALU.mult)
        else:
            P.tt("dve", DG[:, 0:ntaps, :], in0, in1, ALU.mult)

    def post_ln(tiles, l):
        for ti, (c0, n) in enumerate(tiles):
            ps_s = nb()
            ps_q = nb()
            for oc in range(8):
                rb = LB[oc % 2][:, 0:n]
                rq = AB[oc % 2][:, 0:n]
                P.act(rb, X[:, oc, c0:c0 + n], AF.Copy)
                P.act(rq, X[:, oc, c0:c0 + n], AF.Square)
                P.mm(ps_s[:, 0:n], ONES[:, :], rb, start=(oc == 0), stop=(oc == 7))
                P.mm(ps_q[:, 0:n], ONES[:, :], rq, start=(oc == 0), stop=(oc == 7))
            mu = TT[:, 0, 0:n]
            rs = TT[:, 1, 0:n]
            m2 = TT[:, 2, 0:n]
            P.act(mu, ps_s[:, 0:n], AF.Copy, scale=1.0 / 1024)
            P.act(rs, ps_q[:, 0:n], AF.Copy, scale=1.0 / 1024)
            P.tt("dve", m2, mu, mu, ALU.mult)
            P.tt("dve", rs, rs, m2, ALU.subtract)
            P.act(rs, rs, AF.Ln, bias=EPS)
            P.act(rs, rs, AF.Exp, scale=-0.5)
            for oc in range(8):
                t = TT[:, 3 + oc % 2, 0:n]
                xs = X[:, oc, c0:c0 + n]
                P.tt("dve", t, xs, mu, ALU.subtract)
                P.tt("dve", t, t, rs, ALU.mult)
                P.act(xs, t, AF.Identity, bias=LNB[:, oc, l:l + 1], scale=LNG[:, oc, l:l + 1])
                P.copy("dve", XB[:, oc, c0:c0 + n], xs)

    def out_proj_acc(wsrc_rows, Ybuf, ncol_tiles, first):
        P.dma("pool", WOUT[:, :, :], wsrc_rows.rearrange("(k p) n -> p k n", p=128))
        for oc in range(8):
            for (c0, n, y0) in ncol_tiles:
                pb = nb()
                for k in range(4):
                    P.mm(pb[:, 0:n], WOUT[:, k, oc * 128:(oc + 1) * 128], Ybuf[:, k, y0:y0 + n],
                         start=(k == 0), stop=(k == 3))
                xs = X[:, oc, c0:c0 + n]
                if first:
                    P.stt("dve", xs, xs, ALPHA, pb[:, 0:n], ALU.mult, ALU.add)
                else:
                    P.tt("dve", xs, xs, pb[:, 0:n], ALU.add)

    def run_streams(gens):
        gens = list(gens)
        while gens:
            for g in list(gens):
                try:
                    next(g)
                except StopIteration:
                    gens.remove(g)

    def att_stream(sid, segs):
        ps_s, ps_r, ps_o = PB[3 * sid], PB[3 * sid + 1], PB[3 * sid + 2]
        Es = [TT[:, 4 * sid + k, :] for k in range(3)]
        ER = TT[:, 4 * sid + 3, :]
        Lbs = [LB[3 * sid + k] for k in range(3)]
        Ab = AB[sid]
        items = []
        for sg in segs:
            nblk = len(sg["blocks"])
            for j, blk in enumerate(sg["blocks"]):
                items.append((sg, blk, j, j == 0, j == nblk - 1))
        n = len(items)

        def do_qk(k):
            sg, (nk, c0, mfn, bi), j, first, last = items[k]
            if first and sg.get("pre") is not None:
                sg["pre"]()
            sg["qk"](ps_s, bi, nk, c0, j)
        do_qk(0)
        yield
        for i in range(n + 1):
            if i < n:
                sg, (nk, c0, mfn, bi), j, first, last = items[i]
                NQ = sg["NQ"]
                E = Es[i % 3]
                P.act(E[0:nk, c0:NQ], ps_s[0:nk, c0:NQ], AF.Exp)
                if mfn is not None:
                    mfn(E, nk, c0)
                P.act(Lbs[i % 3][0:nk, c0:NQ], E[0:nk, c0:NQ], AF.Ln, bias=1.0)
            yield
            if i >= 1:
                sg, (nk, c0, mfn, bi), j, first, last = items[i - 1]
                NQ = sg["NQ"]
                Lb = Lbs[(i - 1) % 3]
                P.mm(ps_r[:, c0:NQ], NTm[0:nk, :], Lb[0:nk, c0:NQ], start=first, stop=False, skip_group_check=True)
            yield
            if i + 1 < n:
                do_qk(i + 1)
            yield
            if i >= 1:
                P.act(ER[0:nk, c0:NQ], ps_r[0:nk, c0:NQ], AF.Exp)
            yield
            if i >= 1 and not last:
                P.mm(ps_r[:, c0:NQ], NCm[0:nk, :], Lb[0:nk, c0:NQ], start=False, stop=False, skip_group_check=True)
            yield
            if i >= 1:
                E = Es[(i - 1) % 3]
                P.tt("dve", Ab[0:nk, c0:NQ], ER[0:nk, c0:NQ], E[0:nk, c0:NQ], ALU.mult)
                sg["av"](ps_o, Ab, bi, nk, c0, j, first)
                if last:
                    sg["fin"](ps_o)
            yield

    def even_layer(l):
        i = l // 2
        wsrc = w_in_even[i, :, :]
        for g in range(2):
            ld_w(WIN[0], 0, wsrc, (4 * g) * 128)
            ld_w(WIN[0], 1, wsrc, 2048 + (4 * g) * 128)
            for cc in range(4):
                c = 4 * g + cc
                ld_w(WIN[1], 0, wsrc, 1024 + c * 128)
                ld_w(WIN[1], 1, wsrc, 3072 + c * 128)
                bcast_diag(3, WA, ((i * 8 + c) * 3))
                P.copy("dve", UB[:, 0:2], SA[:, i, c, :])
                P.memset("dve", UB[:, 18:20], 0.0)
                for ti, (c0, n) in enumerate(TILES):
                    ps_h = nb()
                    ps_gc = nb()
                    proj_fm(ps_h, WIN[0], 0, c0, n)
                    proj_fm(ps_gc, WIN[0], 128, c0, n)
                    hs = TT[:, ti % 2, 0:n]
                    P.act(hs, ps_h[:, 0:n], AF.Copy)
                    if ti == 0:
                        P.tt("dve", UB[:, 2:18], ps_gc[:, 0:16], hs[:, 0:16], ALU.mult)
                        P.tt("dve", UB[:, 20:36], ps_gc[:, 16:32], hs[:, 16:32], ALU.mult)
                        P.tt("dve", USTA[:, c, 0:2], ps_gc[:, 14:16], hs[:, 14:16], ALU.mult)
                    else:
                        P.tt("dve", UB[:, c0 + 4:c0 + 4 + n], ps_gc[:, 0:n], hs, ALU.mult)
                    if ti == 4:
                        P.tt("dve", USTA[:, c, 2:4], ps_gc[:, 510:512], hs[:, 510:512], ALU.mult)
                    ps_gb = nb()
                    ps_za = nb()
                    proj_fm(ps_gb, WIN[1], 0, c0, n)
                    proj_fm(ps_za, WIN[1], 128, c0, n)
                    ps_cv = nb()
                    if ti == 0:
                        for j in range(3):
                            P.mm(ps_cv[:, 0:16], DG[:, j, :], UB[:, j:j + 16], start=(j == 0), stop=(j == 2))
                        for j in range(3):
                            P.mm(ps_cv[:, 16:32], DG[:, j, :], UB[:, 18 + j:18 + j + 16], start=(j == 0), stop=(j == 2),
                                 skip_group_check=True)
                    else:
                        for j in range(3):
                            P.mm(ps_cv[:, 0:n], DG[:, j, :], UB[:, c0 + 2 + j:c0 + 2 + j + n], start=(j == 0), stop=(j == 2))
                    sz = TT[:, 2 + 2 * (ti % 2), 0:n]
                    g2 = TT[:, 3 + 2 * (ti % 2), 0:n]
                    P.act(sz, ps_za[:, 0:n], AF.Silu)
                    P.tt("dve", g2, ps_gb[:, 0:n], sz, ALU.mult)
                    P.tt("dve", Y4[:, cc, c0:c0 + n], ps_cv[:, 0:n], g2, ALU.mult)
                if cc < 3:
                    ld_w(WIN[0], 0, wsrc, (c + 1) * 128)
                    ld_w(WIN[0], 1, wsrc, 2048 + (c + 1) * 128)
            out_proj_acc(w_out_even[i, (4 * g) * 128:(4 * g + 4) * 128, :], Y4,
                         [(c0, n, c0) for (c0, n) in TILES], first=(g == 0))
        if CFG["sub"] < 1:
            return
        store_T(lambda c: USTA[:, c, 2:4], 2, 1024, cap_o[i, :, :])
        store_T(lambda c: USTA[:, c, 0:2], 2, 1024, cas_o[i, :, :])

        tblocks = [(0, 16), (16, 16)] + [(PO + 128 * j, 128) for j in range(16)]
        if CFG["sub"] < 2:
            return
        for g in range(2):
            for cc in range(4):
                c = 4 * g + cc
                ld_w(WIN[0], 0, wsrc, 4096 + c * 128)
                ld_w(WIN[0], 1, wsrc, 5120 + c * 128)
                ld_w(WIN[1], 0, wsrc, 6144 + c * 128)
                ld_w(WIN[1], 1, wsrc, 7168 + c * 128)
                for (c0, n) in TILES:
                    ps_q = nb()
                    ps_k = nb()
                    proj_fm(ps_q, WIN[0], 0, c0, n)
                    proj_fm(ps_k, WIN[0], 128, c0, n)
                    P.act(QT[:, c0:c0 + n], ps_q[:, 0:n], AF.Copy, scale=0.125)
                    P.copy("dve", KT[:, c0:c0 + n], ps_k[:, 0:n])
                P.copy("dve", QS[0:64, 2 * cc, :], QT[0:64, 0:16])
                P.copy("dve", QS[64:128, 2 * cc + 1, :], QT[64:128, 0:16])
                P.copy("dve", KS[:, cc, :], KT[:, 0:16])
                for which in range(2):
                    wbuf, wc0 = (WIN[0], 128) if which == 0 else (WIN[1], 0)
                    o_p = kp_o if which == 0 else vp_o
                    o_s = ks_o if which == 0 else vs_o
                    for grp in range(5):
                        tbs = [0, 1] if grp == 0 else [2 + 4 * (grp - 1) + k for k in range(4)]
                        pb = nb()
                        for bi2, tb in enumerate(tbs):
                            t0, nt = tblocks[tb]
                            for k in range(8):
                                P.mm(pb[0:nt, bi2 * 128:(bi2 + 1) * 128], XB[:, k, t0:t0 + nt],
                                     wbuf[:, k, wc0:wc0 + 128], start=(k == 0), stop=(k == 7), skip_group_check=True)
                        stg = TT[:, 6 + (grp % 2), :]
                        if grp == 0:
                            P.copy("dve", stg[0:16, 0:256], pb[0:16, 0:256])
                            for hh in range(2):
                                h = 2 * c + hh
                                P.dma("sp", o_s[i, h, :, :], stg[0:16, hh * 64:(hh + 1) * 64])
                                P.dma("sp", o_p[i, h, 0:16, :], stg[0:16, 128 + hh * 64:128 + (hh + 1) * 64])
                            if which == 1:
                                P.copy("dve", VB[0:16, 0, :], stg[0:16, 0:128])
                                P.copy("dve", VB[0:16, 1, :], stg[0:16, 128:256])
                                P.copy("dve", VS[0:16, cc, :], stg[0:16, 0:128])
                        else:
                            P.copy("dve", stg[:, :], pb[:, :])
                            q0 = 16 + 512 * (grp - 1)
                            for hh in range(2):
                                h = 2 * c + hh
                                P.dma("sp", o_p[i, h, q0:q0 + 512, :].rearrange("(j p) d -> p j d", p=128),
                                      stg.rearrange("p (j f) -> p j f", j=4)[:, :, hh * 64:(hh + 1) * 64])
                            if which == 1:
                                tb0 = 2 + 4 * (grp - 1)
                                P.copy("dve", VB[:, tb0:tb0 + 4, :], stg.rearrange("p (j f) -> p j f", j=4))
                for (c0, n) in TILES:
                    pz = nb()
                    proj_fm(pz, WIN[1], 128, c0, n)
                    P.act(SZA[:, c0:c0 + n], pz[:, 0:n], AF.Silu)
                P.copy("dve", SZS[:, cc, :], SZA[:, 0:16])

                def mk_stream(hh, cc=cc):
                    rows = slice(hh * 64, (hh + 1) * 64)
                    segs = []

                    def qk_m(ps_s, bi, nk, c0, j):
                        P.mm(ps_s[0:16, 0:16], KT[rows, 16:32], QT[rows, 16:32], start=True, stop=True)

                    def av_m(ps_o, Ab, bi, nk, c0, j, first):
                        P.mm(ps_o[:, 0:16], VB[0:16, 1, :], Ab[0:16, 0:16], start=first, stop=False, skip_group_check=True)

                    def mfn_m(E, nk, c0):
                        P.tt("dve", E[0:16, 0:16], E[0:16, 0:16], M0[0:16, 0:16], ALU.mult)

                    def fin_m(ps_o):
                        P.tt("dve", Y4[rows, cc, 16:32], ps_o[rows, 0:16], SZA[rows, 16:32], ALU.mult)
                    segs.append(dict(NQ=16, blocks=[(16, 0, mfn_m, "M")], qk=qk_m, av=av_m, fin=fin_m, pre=None))
                    for qi in range(4):
                        qc0 = PO + 512 * qi
                        SZ = SZA[:, qc0:qc0 + 512]

                        def mfn(E, nk, c0):
                            P.tt("dve", E[:, c0:512], E[:, c0:512], M0[:, 0:512 - c0], ALU.mult)
                        blocks = []
                        for k in (3, 2, 1, 0):
                            blocks.append((128, 128 * k, mfn, 4 * qi + k))
                        for j in range(4 * qi - 1, -1, -1):
                            blocks.append((128, 0, None, j))
                        blocks.append((16, 0, None, "M"))

                        def qk(ps_s, bi, nk, c0, j, qc0=qc0):
                            kc0 = 16 if bi == "M" else PO + 128 * bi
                            P.mm(ps_s[0:nk, c0:512], KT[rows, kc0:kc0 + nk], QT[rows, qc0 + c0:qc0 + 512],
                                 start=True, stop=True)

                        def av(ps_o, Ab, bi, nk, c0, j, first):
                            tb = 1 if bi == "M" else 2 + bi
                            P.mm(ps_o[:, c0:512], VB[0:nk, tb, :], Ab[0:nk, c0:512], start=first, stop=False,
                                 skip_group_check=True)

                        def fin(ps_o, qc0=qc0, SZ=SZ):
                            P.tt("dve", Y4[rows, cc, qc0:qc0 + 512], ps_o[rows, :], SZ[rows, :], ALU.mult)
                        segs.append(dict(NQ=512, blocks=blocks, qk=qk, av=av, fin=fin, pre=None))
                    return att_stream(hh, segs)
                run_streams([mk_stream(0), mk_stream(1)])

            def sample_stream():
                NQ = 128
                blocks = [(16, 0, "S", "S")] + [(16 if b == 32 else 128, 0, None, b) for b in range(32, -1, -1)][:CFG["sblk"]]

                def qk(ps_s, bi, nk, c0, j):
                    if bi == "S":
                        for hq in range(8):
                            cq, hh = hq // 2, hq % 2
                            P.mm(ps_s[0:16, hq * 16:(hq + 1) * 16], KS[:, cq, :], QS[:, hq, :], start=True, stop=True,
                                 skip_group_check=True)
                        return
                    r0 = 128 * bi

                    def kdma(jj):
                        nkk, _, _, bb = blocks[jj]
                        P.dma("sp", KC[jj % 2][0:nkk, :].rearrange("s (h d) -> s h d", h=8),
                              ck[i, 8 * g:8 * g + 8, 128 * bb:128 * bb + nkk, :].rearrange("h s d -> s h d"))
                    if j == 1:
                        kdma(1)
                    P.dma("pool", VC[j % 4][0:nk, :].rearrange("s (h d) -> s h d", h=8),
                          cv[i, 8 * g:8 * g + 8, r0:r0 + nk, :].rearrange("h s d -> s h d"))
                    pt = PB[7]
                    for cq in range(4):
                        P.transpose(pt[:, cq * 128:cq * 128 + nk], KC[j % 2][0:nk, cq * 128:(cq + 1) * 128], IDF[0:nk, 0:nk])
                    P.copy("dve", KTC[:, :, 0:nk], pt[:, :].rearrange("p (c s) -> p c s", c=4)[:, :, 0:nk])
                    if j + 1 < len(blocks):
                        kdma(j + 1)
                    for hq in range(8):
                        cq, hh = hq // 2, hq % 2
                        P.mm(ps_s[0:nk, hq * 16:(hq + 1) * 16], KTC[:, cq, 0:nk], QS[:, hq, :], start=True, stop=True,
                             skip_group_check=True)

                def av(ps_o, Ab, bi, nk, c0, j, first):
                    for hq in range(8):
                        cq = hq // 2
                        if bi == "S":
                            lhs = VS[0:16, cq, :]
                        else:
                            lhs = VC[j % 4][0:nk, cq * 128:(cq + 1) * 128]
                        P.mm(ps_o[:, hq * 16:(hq + 1) * 16], lhs, Ab[0:nk, hq * 16:(hq + 1) * 16],
                             start=(first and hq == 0), stop=False, skip_group_check=True)

                def mfn(E, nk, c0):
                    m = bass.AP(M0, 0, [[512, 16], [0, 8], [1, 16]])
                    ev = E[0:16, 0:128].rearrange("p (h t) -> p h t", h=8)
                    P.tt("dve", ev, ev, m, ALU.mult)

                def fin(ps_o):
                    for hq in range(8):
                        cq, hh = hq // 2, hq % 2
                        rows = slice(hh * 64, (hh + 1) * 64)
                        P.tt("dve", Y4[rows, cq, 0:16], ps_o[rows, hq * 16:(hq + 1) * 16], SZS[rows, cq, :], ALU.mult)
                blocks = [(nk, c0, (mfn if m else None), bi) for (nk, c0, m, bi) in blocks]
                return att_stream(0, [dict(NQ=NQ, blocks=blocks, qk=qk, av=av, fin=fin, pre=None)])
            if CFG["sub"] < 5:
                continue
            run_streams([sample_stream()])
            if CFG["sub"] < 6:
                continue
            out_proj_acc(w_out_even[i, 1024 + (4 * g) * 128:1024 + (4 * g + 4) * 128, :], Y4,
                         [(c0, n, c0) for (c0, n) in TILES], first=False)
        if CFG["sub"] < 7:
            return
        post_ln(TILES, l)

    def odd_layer(l):
        i = l // 2
        wsrc = w_in_odd[i, :, :]
        groups = [[TILES[0], TILES[1]], [TILES[2]], [TILES[3]], [TILES[4]]]
        for gi, gt in enumerate(groups):
            gc0 = gt[0][0]
            ncols = sum(n for _, n in gt)
            nst = len(gt)
            st["reserved"] = set()
            ps_sum = [PB[0], PB[1]]
            ps_sq = [PB[2], PB[1]]
            st["reserved"] = {0, 1, 2}
            ld_w(WIN[0], 0, wsrc, 0)
            ld_w(WIN[0], 1, wsrc, 2048)
            ld_w(WIN[1], 0, wsrc, 128)
            ld_w(WIN[1], 1, wsrc, 2048 + 128)
            items = [(c, ti) for c in range(16) for ti in range(nst)]
            pend = {}

            def proj_item(k):
                c, ti = items[k]
                c0, n = gt[ti]
                ps_a = nb()
                ps_g = nb()
                proj_fm(ps_a, WIN[c % 2], 0, c0, n)
                proj_fm(ps_g, WIN[c % 2], 128, c0, n)
                pend[k] = (ps_a, ps_g)
                if ti == nst - 1 and c + 2 < 16:
                    ld_w(WIN[c % 2], 0, wsrc, (c + 2) * 128)
                    ld_w(WIN[c % 2], 1, wsrc, 2048 + (c + 2) * 128)
            proj_item(0)
            for k, (c, ti) in enumerate(items):
                if k + 1 < len(items):
                    proj_item(k + 1)
                c0, n = gt[ti]
                ps_a, ps_g = pend.pop(k)
                if ti == 0:
                    bcast_diag(31, WC, ((i * 16 + c) * 31))
                    if gi == 0:
                        P.copy("dve", UB[:, 0:30], SC[:, i, c, :])
                        P.memset("dve", UB[:, 64:94], 0.0)
                    else:
                        P.copy("dve", UB[:, 64:94], HIST[:, c, :])
                sg = TT[:, k % 4, 0:n]
                P.act(sg, ps_g[:, 0:n], AF.Sigmoid)
                lc0 = c0 - gc0
                if gi == 0 and ti == 0:
                    P.tt("dve", UB[:, 30:46], ps_a[:, 0:16], sg[:, 0:16], ALU.mult)
                    P.tt("dve", UB[:, 94:110], ps_a[:, 16:32], sg[:, 16:32], ALU.mult)
                    P.tt("dve", USTS[:, c, :], ps_a[:, 0:16], sg[:, 0:16], ALU.mult)
                    ub0 = None
                else:
                    ub0 = 94 + (lc0 - 16 if gi == 0 else lc0)
                    P.tt("dve", UB[:, ub0:ub0 + n], ps_a[:, 0:n], sg, ALU.mult)
                if gi == 3:
                    P.tt("dve", USTP[:, c, :], ps_a[:, 482:512], sg[:, 482:512], ALU.mult)
                ps_cv = nb()
                if ub0 is None:
                    for j in range(31):
                        P.mm(ps_cv[:, 0:16], DG[:, j, :], UB[:, j:j + 16], start=(j == 0), stop=(j == 30))
                    for j in range(31):
                        P.mm(ps_cv[:, 16:32], DG[:, j, :], UB[:, 64 + j:64 + j + 16], start=(j == 0), stop=(j == 30),
                             skip_group_check=True)
                else:
                    for j in range(31):
                        P.mm(ps_cv[:, 0:n], DG[:, j, :], UB[:, ub0 - 30 + j:ub0 - 30 + j + n],
                             start=(j == 0), stop=(j == 30))
                cs = CC[:, c, lc0:lc0 + n]
                P.act(cs, ps_cv[:, 0:n], AF.Identity, bias=CVB[:, c, i:i + 1])
                cb = LB[k % 2][:, 0:n]
                cq = AB[k % 2][:, 0:n]
                P.act(cb, cs, AF.Copy)
                P.act(cq, cs, AF.Square)
                if nst == 2 and ti == 0:
                    P.mm(PB[1][:, 0:32], ONES[:, :], cb, start=(c == 0), stop=(c == 15), skip_group_check=True)
                    P.mm(PB[1][:, 32:64], ONES[:, :], cq, start=False, stop=(c == 15), skip_group_check=True)
                else:
                    P.mm(PB[0][:, 0:n], ONES[:, :], cb, start=(c == 0), stop=(c == 15), skip_group_check=True)
                    P.mm(PB[2][:, 0:n], ONES[:, :], cq, start=(c == 0), stop=(c == 15), skip_group_check=True)
                if gi < 3 and ti == nst - 1:
                    lastcol = 94 + (ncols - 16 if gi == 0 else ncols)
                    P.copy("dve", HIST[:, c, :], UB[:, lastcol - 30:lastcol])
            stats = []
            for ti, (c0, n) in enumerate(gt):
                if n == 32:
                    mu, rs = ST0[:, 0, :], ST0[:, 1, :]
                else:
                    mu, rs = SZT[:, 0, 0:n], SZT[:, 1, 0:n]
                if nst == 2 and ti == 0:
                    s_ap, q_ap = PB[1][:, 0:32], PB[1][:, 32:64]
                else:
                    s_ap, q_ap = PB[0][:, 0:n], PB[2][:, 0:n]
                P.act(mu, s_ap, AF.Copy, scale=1.0 / 2048)
                P.act(rs, q_ap, AF.Copy, scale=1.0 / 2048)
                m2 = TT[:, 0, 0:n]
                P.tt("dve", m2, mu, mu, ALU.mult)
                P.tt("dve", rs, rs, m2, ALU.subtract)
                P.act(rs, rs, AF.Ln, bias=EPS)
                P.act(rs, rs, AF.Exp, scale=-0.5)
                stats.append((mu, rs))
            st["reserved"] = set()
            ld_w(WIN[0], 0, wsrc, 4096)
            ld_w(WIN[0], 1, wsrc, 4096 + 128)

            for c in range(16):
                wb = WIN[(c // 2) % 2]
                if c % 2 == 0 and c < 14:
                    ld_w(WIN[(c // 2 + 1) % 2], 0, wsrc, 4096 + (c + 2) * 128)
                    ld_w(WIN[(c // 2 + 1) % 2], 1, wsrc, 4096 + (c + 3) * 128)
                for ti, (c0, n) in enumerate(gt):
                    lc0 = c0 - gc0
                    mu, rs = stats[ti]
                    pz = nb()
                    proj_fm(pz, wb, (c % 2) * 128, c0, n)
                    sz = TT[:, 3 * (c % 2), 0:n]
                    P.act(sz, pz[:, 0:n], AF.Silu)
                    t = TT[:, 3 * (c % 2) + 1, 0:n]
                    cs = CC[:, c, lc0:lc0 + n]
                    P.tt("dve", t, cs, mu, ALU.subtract)
                    P.tt("dve", t, t, rs, ALU.mult)
                    s2 = TT[:, 3 * (c % 2) + 2, 0:n]
                    P.act(s2, t, AF.Silu, bias=LCB[:, c, i:i + 1], scale=LCG[:, c, i:i + 1])
                    P.tt("dve", YO[:, c % 4, lc0:lc0 + n], s2, sz, ALU.mult)
                if c % 4 == 3:
                    k0 = (c // 4) * 4
                    out_proj_acc(w_out_odd[i, k0 * 128:(k0 + 4) * 128, :], YO,
                                 [(c0, n, c0 - gc0) for (c0, n) in gt], first=(c == 3))
            st["reserved"] = set()
            post_ln(gt, l)
        store_T(lambda c: USTP[:, c, :], 30, 2048, ccp_o[i, :, :])
        P.dma("sp", ccs_o[i, 0:14, :], scc[i, 16:30, :])
        store_T(lambda c: USTS[:, c, :], 16, 2048, ccs_o[i, 14:30, :])

    for l in range(CFG["layers"]):
        if l % 2 == 0:
            even_layer(l)
        else:
            odd_layer(l)

    for bi, (r0, nr) in enumerate(blocks_in if CFG["pre"] & 8 else []):
        for g in range(2):
            pb = nb()
            for cc in range(4):
                P.transpose(pb[0:nr, cc * 128:(cc + 1) * 128], X[:, 4 * g + cc, r0:r0 + nr], IDF[:, :])
            P.copy("dve", TT[0:nr, (bi % 2) * 2 + g, :], pb[0:nr, :])
        P.dma("sp", y_o[r0:r0 + nr, :].rearrange("r (a b) -> r a b", b=512), TT[0:nr, (bi % 2) * 2:(bi % 2) * 2 + 2, :])

    P.finish()
    return nc


_NC_CACHE = {}


def kernel(x_prompt, x_sample, cache_sb_k, cache_sb_v, state_conv_a, state_conv_c, meta_tokens,
           w_in_even, conv_a_w, w_out_even, w_in_odd, conv_c_w, conv_c_b, ln_c_g, ln_c_b,
           w_out_odd, post_ln_g, post_ln_b):
    f = lambda a: np.ascontiguousarray(np.asarray(a, dtype=np.float32))
    x_prompt, x_sample, cache_sb_k, cache_sb_v = f(x_prompt), f(x_sample), f(cache_sb_k), f(cache_sb_v)
    state_conv_a, state_conv_c, meta_tokens = f(state_conv_a), f(state_conv_c), f(meta_tokens)
    shared = {"w_in_even": f(w_in_even), "conv_a_w": f(conv_a_w), "w_out_even": f(w_out_even),
              "w_in_odd": f(w_in_odd), "conv_c_w": f(conv_c_w), "conv_c_b": f(conv_c_b), "ln_c_g": f(ln_c_g),
              "ln_c_b": f(ln_c_b), "w_out_odd": f(w_out_odd), "post_ln_g": f(post_ln_g), "post_ln_b": f(post_ln_b)}
    if "nc" not in _NC_CACHE:
        _NC_CACHE["nc"] = build_program()
    nc = _NC_CACHE["nc"]
    in_maps = []
    for b in range(NCORES):
        m = dict(shared)
        m["xin"] = np.ascontiguousarray(np.concatenate([x_sample[b], meta_tokens, x_prompt[b]], axis=0))
        m["ck"] = np.ascontiguousarray(cache_sb_k[:, b])
        m["cv"] = np.ascontiguousarray(cache_sb_v[:, b])
        m["sca"] = np.ascontiguousarray(state_conv_a[:, b])
        m["scc"] = np.ascontiguousarray(state_conv_c[:, b])
        in_maps.append(m)
    res = run_bass_kernel_spmd(nc, in_maps, core_ids=list(range(NCORES)))
    R = res.results
    y = np.stack([r["y"] for r in R], 0)
    y_prompt = np.ascontiguousarray(y[:, PO:, :])
    y_sample = np.ascontiguousarray(y[:, 0:16, :])
    st = lambda k: np.ascontiguousarray(np.stack([r[k] for r in R], 1)).astype(np.float32)
    return (y_prompt, y_sample, st("kp"), st("vp"), st("cap"), st("ccp"), st("ks"), st("vs"), st("cas"), st("ccs"))
```
